# Optimizing a Trainium2 kernel written in Bass

```python
import math
import jax, jax.numpy as jnp
from jax import lax
import numpy as np

D_MODEL = 1024
BATCH = 2
SEQ = 8192
DEPTH = 1

HEAD_DIM = 64
N_HEADS = D_MODEL // HEAD_DIM
SB_HEADS = N_HEADS // 2
SWA_HEADS = N_HEADS - SB_HEADS
SWA_KV_HEADS = 2
SWA_GROUP = SWA_HEADS // SWA_KV_HEADS
WINDOW = 128
BLOCK_Q = 128
SB_W = SB_HEADS * HEAD_DIM
SWA_QW = SWA_HEADS * HEAD_DIM
SWA_KW = SWA_KV_HEADS * HEAD_DIM
D_IN = 3 * SB_W + SWA_QW + 2 * SWA_KW
D_FF = ((-(-8 * D_MODEL // 3)) + 255) // 256 * 256
N_MOD = 6
DEEPNORM_ALPHA = (2.0 * DEPTH) ** 0.25
DEEPNORM_BETA = (8.0 * DEPTH) ** -0.25
LN_EPS = 1e-5
RMS_EPS = 1e-6
MASK_VALUE = -1e30

kernel_name = "hymba_stickbreak_swa_sink_deepnorm_adaln"


def layer_norm(x, g, b):
    xf = x.astype(jnp.float32)
    mu = jnp.mean(xf, axis=-1, keepdims=True)
    var = jnp.mean(jnp.square(xf - mu), axis=-1, keepdims=True)
    return ((xf - mu) * lax.rsqrt(var + LN_EPS)).astype(x.dtype) * g + b


def rms_norm(x, g):
    xf = x.astype(jnp.float32)
    ms = jnp.mean(jnp.square(xf), axis=-1, keepdims=True)
    return (xf * lax.rsqrt(ms + RMS_EPS)).astype(x.dtype) * g


def alibi_slopes(n_heads):
    return jnp.exp2(-8.0 * jnp.arange(1, n_heads + 1, dtype=jnp.float32) / n_heads)


def stick_breaking_attention(q, k, v):
    B, S, H, Dh = q.shape
    nblk = S // BLOCK_Q
    scale = 1.0 / math.sqrt(Dh)
    qb = q.reshape(B, nblk, BLOCK_Q, H, Dh).transpose(1, 0, 3, 2, 4)
    kpos = jnp.arange(S)

    def one_block(args):
        qi, i = args
        z = jnp.einsum('bhqd,bshd->bhqs', qi, k).astype(jnp.float32) * scale
        qpos = i * BLOCK_Q + jnp.arange(BLOCK_Q)
        before = kpos[None, :] < qpos[:, None]
        log_beta = jax.nn.log_sigmoid(z)
        log_rem = jnp.where(before, jax.nn.log_sigmoid(-z), 0.0)
        suffix = lax.cumsum(log_rem, axis=3, reverse=True) - log_rem
        w = jnp.where(before, jnp.exp(log_beta + suffix), 0.0)
        return jnp.einsum('bhqs,bshd->bqhd', w.astype(v.dtype), v)

    out = lax.map(one_block, (qb, jnp.arange(nblk)))
    return out.transpose(1, 0, 2, 3, 4).reshape(B, S, H * Dh)


def sliding_window_sink_attention(q, k, v, sinks):
    B, S, Hq, Dh = q.shape
    nblk = S // WINDOW
    qb = q.reshape(B, nblk, WINDOW, SWA_KV_HEADS, SWA_GROUP, Dh)

    def banded(t):
        tb = t.reshape(B, nblk, WINDOW, SWA_KV_HEADS, Dh)
        prev = jnp.pad(tb, ((0, 0), (1, 0), (0, 0), (0, 0), (0, 0)))[:, :-1]
        return jnp.concatenate([prev, tb], axis=2)

    kb, vb = banded(k), banded(v)
    s = jnp.einsum('bnqkgd,bnskd->bnkgqs', qb, kb).astype(jnp.float32) / math.sqrt(Dh)
    qi = jnp.arange(WINDOW)
    kj = jnp.arange(2 * WINDOW)
    dist = (qi[:, None] + WINDOW - kj[None, :]).astype(jnp.float32)
    in_band = (dist >= 0) & (dist < WINDOW)
    key_pos = jnp.arange(nblk)[:, None] * WINDOW - WINDOW + kj[None, :]
    mask = in_band[None, :, :] & (key_pos >= 0)[:, None, :]
    slopes = alibi_slopes(SWA_HEADS).reshape(SWA_KV_HEADS, SWA_GROUP)
    s = s - slopes[None, None, :, :, None, None] * dist[None, None, None, None]
    s = jnp.where(mask[None, :, None, None], s, MASK_VALUE)
    sink = sinks.astype(jnp.float32).reshape(SWA_KV_HEADS, SWA_GROUP)[None, None, :, :, None, None]
    m = jnp.maximum(jnp.max(s, axis=-1, keepdims=True), sink)
    p = jnp.exp(s - m)
    p = p / (jnp.sum(p, axis=-1, keepdims=True) + jnp.exp(sink - m))
    o = jnp.einsum('bnkgqs,bnskd->bnqkgd', p.astype(v.dtype), vb)
    return o.reshape(B, S, Hq * Dh)


def setup_inputs(seed: int = 0) -> dict:
    key = jax.random.key(seed)
    ks = jax.random.split(key, 16)
    f32 = jnp.float32
    nrm = lambda k, shape: jax.random.normal(k, shape, f32)
    return {
        "x": nrm(ks[0], (BATCH, SEQ, D_MODEL)),
        "c": nrm(ks[1], (BATCH, D_MODEL)),
        "w_ada": nrm(ks[2], (DEPTH, D_MODEL, N_MOD * D_MODEL)) * (0.1 * D_MODEL ** -0.5),
        "b_ada": nrm(ks[3], (DEPTH, N_MOD * D_MODEL)) * 0.01,
        "w_in": nrm(ks[4], (DEPTH, D_MODEL, D_IN)) * D_MODEL ** -0.5,
        "b_in": nrm(ks[5], (DEPTH, D_IN)) * 0.01,
        "sinks": nrm(ks[6], (DEPTH, SWA_HEADS)) * 0.5,
        "gn_sb": 1.0 + 0.01 * nrm(ks[7], (DEPTH, SB_W)),
        "gn_swa": 1.0 + 0.01 * nrm(ks[8], (DEPTH, SWA_QW)),
        "w_out": nrm(ks[9], (DEPTH, D_MODEL, D_MODEL)) * (DEEPNORM_BETA * D_MODEL ** -0.5),
        "ln1_g": 1.0 + 0.01 * nrm(ks[10], (DEPTH, D_MODEL)),
        "ln1_b": 0.01 * nrm(ks[11], (DEPTH, D_MODEL)),
        "w_gu": nrm(ks[12], (DEPTH, D_MODEL, 2 * D_FF)) * D_MODEL ** -0.5,
        "w_down": nrm(ks[13], (DEPTH, D_FF, D_MODEL)) * (DEEPNORM_BETA * D_FF ** -0.5),
        "ln2_g": 1.0 + 0.01 * nrm(ks[14], (DEPTH, D_MODEL)),
        "ln2_b": 0.01 * nrm(ks[15], (DEPTH, D_MODEL)),
    }


def reference(x, c, w_ada, b_ada, w_in, b_in, sinks, gn_sb, gn_swa, w_out,
              ln1_g, ln1_b, w_gu, w_down, ln2_g, ln2_b):
    B, S, _ = x.shape
    for l in range(DEPTH):
        mod = jax.nn.silu(c) @ w_ada[l] + b_ada[l]
        sh_a, sc_a, g_a, sh_f, sc_f, g_f = jnp.split(mod[:, None, :], N_MOD, axis=-1)

        h = x * (1.0 + sc_a) + sh_a
        proj = h @ w_in[l] + b_in[l]
        o0, o1, o2, o3, o4 = np.cumsum([SB_W, SB_W, SB_W, SWA_QW, SWA_KW])
        q_sb = proj[..., :o0].reshape(B, S, SB_HEADS, HEAD_DIM)
        k_sb = proj[..., o0:o1].reshape(B, S, SB_HEADS, HEAD_DIM)
        v_sb = proj[..., o1:o2].reshape(B, S, SB_HEADS, HEAD_DIM)
        q_sw = proj[..., o2:o3].reshape(B, S, SWA_HEADS, HEAD_DIM)
        k_sw = proj[..., o3:o4].reshape(B, S, SWA_KV_HEADS, HEAD_DIM)
        v_sw = proj[..., o4:].reshape(B, S, SWA_KV_HEADS, HEAD_DIM)

        y_sb = stick_breaking_attention(q_sb, k_sb, v_sb)
        y_sw = sliding_window_sink_attention(q_sw, k_sw, v_sw, sinks[l])
        mixed = jnp.concatenate([rms_norm(y_sb, gn_sb[l]), rms_norm(y_sw, gn_swa[l])], axis=-1)
        attn = mixed @ w_out[l]
        x = layer_norm(DEEPNORM_ALPHA * x + (1.0 + g_a) * attn, ln1_g[l], ln1_b[l])

        h = x * (1.0 + sc_f) + sh_f
        gate, up = jnp.split(h @ w_gu[l], 2, axis=-1)
        ffn = (jax.nn.silu(gate) * up) @ w_down[l]
        x = layer_norm(DEEPNORM_ALPHA * x + (1.0 + g_f) * ffn, ln2_g[l], ln2_b[l])
    return x
```

```python
import numpy as np
import ml_dtypes
import concourse.bass as bass
import concourse.mybir as mybir
from concourse.bass_utils import run_bass_kernel_spmd

F32 = mybir.dt.float32
BF16 = mybir.dt.bfloat16
AF = mybir.ActivationFunctionType
ALU = mybir.AluOpType
AX = mybir.AxisListType

D = 1024
S = 8192
NB = 64
OWN = 2048
DFF = 2816
NFC = 22
ALPHA = 2.0 ** 0.25
MASKV = -240.0
NEG_BIG = -30000.0
ARENA_F32 = 51200


class Op:
    __slots__ = ("id", "eng", "emit", "deps", "is_dma", "sem", "val", "sig")

    def __init__(self, id, eng, emit, deps, is_dma):
        self.id = id
        self.eng = eng
        self.emit = emit
        self.deps = deps
        self.is_dma = is_dma
        self.sem = None
        self.val = 0
        self.sig = False


class Prog:
    ENGS = ("pe", "act", "dve", "pool", "sp")
    EPOCH = 6000
    NDMA = 20

    def __init__(self):
        self.ops = []
        self.last_w = {}
        self.readers = {}
        self.dma_readers = {}
        self.eng_last = {e: None for e in self.ENGS}
        self.pending = {e: set() for e in self.ENGS}
        self.dma_open = []
        self.dma_last_on_sem = [None] * self.NDMA
        self.dma_rr = 0
        self.dma_uses = [0] * self.NDMA

    def add(self, eng, emit, reads=(), writes=(), dma=False):
        deps = set()
        for k in reads:
            w = self.last_w.get(k)
            if w is not None:
                deps.add(w)
        for k in writes:
            w = self.last_w.get(k)
            if w is not None:
                deps.add(w)
            deps |= set(self.readers.get(k, {}).values())
            deps |= self.dma_readers.get(k, set())
        deps |= self.pending[eng]
        self.pending[eng] = set()
        op = Op(len(self.ops), eng, emit, deps, dma)
        if dma:
            s = self.dma_rr
            self.dma_rr = (self.dma_rr + 1) % self.NDMA
            prev = self.dma_last_on_sem[s]
            if prev is not None:
                op.deps.add(prev)
            self.dma_last_on_sem[s] = op.id
            self.dma_uses[s] += 1
            op.sem = ("dma", s)
            op.val = 16 * self.dma_uses[s]
            op.sig = True
            self.dma_open.append(op.id)
        op.deps.discard(op.id)
        self.ops.append(op)
        for k in reads:
            if dma:
                self.dma_readers.setdefault(k, set()).add(op.id)
            else:
                self.readers.setdefault(k, {})[eng] = op.id
        for k in writes:
            self.last_w[k] = op.id
            self.readers[k] = {}
            self.dma_readers[k] = set()
        if not dma:
            self.eng_last[eng] = op.id
        return op.id

    def barrier(self):
        deps = set(self.dma_open)
        self.dma_open = []
        for e in self.ENGS:
            if self.eng_last[e] is not None:
                deps.add(self.eng_last[e])
        for e in self.ENGS:
            self.pending[e] |= deps

    def finalize(self):
        ops = self.ops
        for op in ops:
            for d in op.deps:
                p = ops[d]
                if p.is_dma:
                    continue
                if p.eng == "pe" and op.eng == "pe" and not op.is_dma:
                    continue
                p.sig = True
        cnt = {e: 0 for e in self.ENGS}
        self.n_epochs = {e: 1 for e in self.ENGS}
        for op in ops:
            if op.is_dma or not op.sig:
                continue
            c = cnt[op.eng]
            ep = c // self.EPOCH
            op.sem = (op.eng, ep)
            op.val = c - ep * self.EPOCH + 1
            cnt[op.eng] = c + 1
            self.n_epochs[op.eng] = ep + 1

    def sem_names(self):
        names = [("dma", i) for i in range(self.NDMA)]
        for e in self.ENGS:
            for ep in range(self.n_epochs[e]):
                names.append((e, ep))
        return names

    def emit_engine(self, eng, engine, sems):
        waited = {}
        ops = self.ops
        for op in ops:
            if op.eng != eng:
                continue
            need = {}
            for d in op.deps:
                p = ops[d]
                if (not p.is_dma) and p.eng == "pe" and eng == "pe" and not op.is_dma:
                    continue
                if p.sem is None:
                    continue
                if need.get(p.sem, 0) < p.val:
                    need[p.sem] = p.val
            for sname, v in need.items():
                if waited.get(sname, 0) >= v:
                    continue
                engine.wait_ge(sems[sname], v)
                waited[sname] = v
            if op.emit is None:
                continue
            ins = op.emit(engine)
            if op.sig:
                ins.then_inc(sems[op.sem], 16 if op.is_dma else 1)


class Arena:
    def __init__(self, ap, nbytes):
        self.ap = ap
        self.top = 0
        self.cap = nbytes
        self.peak = 0

    def alloc(self, shape, dtype):
        n = 1
        for s in shape[1:]:
            n *= s
        esz = 2 if dtype == BF16 else 4
        nb = (n * esz + 31) // 32 * 32
        off = self.top
        self.top += nb
        self.peak = max(self.peak, self.top)
        assert self.top <= self.cap, ("arena overflow", self.top, self.cap)
        v = self.ap[:, off // 4:(off + nb) // 4]
        if dtype == BF16:
            v = v.bitcast(BF16)
        v = v[:, 0:n]
        if len(shape) == 3:
            v = v.rearrange("p (a b) -> p a b", a=shape[1])
        elif len(shape) == 4:
            v = v.rearrange("p (a b c) -> p a b c", a=shape[1], b=shape[2])
        return v

    def mark(self):
        return self.top

    def release(self, m):
        self.top = m


def build_program(stop=None):
    nc = bass.Bass("TRN2", target_bir_lowering=False)

    def din(name, shape, dt=F32):
        return nc.dram_tensor(name, list(shape), dt, kind="ExternalInput").ap()

    early = stop is not None and (stop in ("mod", "projA", "swa") or stop.startswith("swa"))
    xT = din("xT", [D, 128 if early else S])
    xq = din("xq", [D, OWN])
    xh = din("xh", [D, OWN])
    w_ada = din("w_ada", [D, 6 * D])
    w_in = din("w_in", [D, 2304])
    w_out = din("w_out", [D, 128 if early else D])
    w_gu = din("w_gu", [D, 128 if early else 2 * DFF])
    w_down = din("w_down", [DFF, 128 if early else D])
    cst_d = din("cst", [128, 5, 128], BF16)
    mk_d = din("mk", [128, 4, 128], BF16)
    vec_d = din("vec", [128, 128])
    bvb_d = din("bvb", [128, 768])
    swab_d = din("swab", [128, 2, 8, 256])
    outT = nc.dram_tensor("outT", [D, OWN], F32, kind="ExternalOutput").ap()

    arena_t = nc.alloc_sbuf_tensor("arena", [128, ARENA_F32], F32)
    A = Arena(arena_t[:, :], ARENA_F32 * 4)
    ps = nc.alloc_psum_tensor("ps", [128, 4096], F32)

    def bank(b, n=1):
        return ps[:, b * 512:(b + n) * 512]

    P = Prog()

    def finish():
        P.finalize()
        names = P.sem_names()
        from contextlib import ExitStack
        with ExitStack() as es:
            sems = {}
            for nm in names:
                sems[nm] = es.enter_context(nc.semaphore("s_%s_%d" % nm))
            block = es.enter_context(nc.Block())

            @block.tensor
            def _(e):
                P.emit_engine("pe", e, sems)

            @block.scalar
            def _(e):
                P.emit_engine("act", e, sems)

            @block.vector
            def _(e):
                P.emit_engine("dve", e, sems)

            @block.gpsimd
            def _(e):
                P.emit_engine("pool", e, sems)

            @block.sync
            def _(e):
                P.emit_engine("sp", e, sems)
        return nc, P, A


    outT_dbg = outT.rearrange("(kc p) t -> p kc t", p=128)

    def dbg_stop(dumps):
        P.barrier()
        col = 0
        keys = []
        for i, (v, n) in enumerate(dumps):
            P.add("sp", lambda e, v=v, n=n, col=col: e.dma_start(out=outT_dbg[:, 0, col:col + n], in_=v), [], [("dbg", i)], dma=True)
            keys.append(("dbg", i))
            col += n
        P.add("sp", None, reads=keys, writes=[])
        return finish()

    cst = A.alloc([128, 5, 128], BF16)
    ident = cst[:, 0, :]
    Umat = cst[:, 1, :]
    negones = cst[:, 2, :]
    onesM = cst[:, 3, :]
    ones512 = cst[:, 4, :]
    Mk = A.alloc([128, 4, 128], BF16)
    vec = A.alloc([128, 128], F32)
    b_in_c = vec[:, 0:18]
    gn_c = vec[:, 18:26]
    ln1g = vec[:, 26:34]
    ln1b = vec[:, 34:42]
    ln2g = vec[:, 42:50]
    ln2b = vec[:, 50:58]
    sinks_c = vec[:, 58:66]
    b_ada_c = vec[:, 66:114]
    c_c = vec[:, 114:122]
    modT = A.alloc([128, 48], F32)
    onep = A.alloc([128, 48], F32)
    silu_c = A.alloc([128, 8], F32)
    tmp8 = A.alloc([128, 8], F32)
    A_small_bf = A.alloc([128, 8], BF16)
    bvb = A.alloc([128, 768], F32)
    bq8 = A.alloc([128, 18], F32)
    wstage_flat = [A.alloc([128, 2048], F32) for _ in range(2)]
    wstage = [w.rearrange("p (a b) -> p a b", a=8) for w in wstage_flat]
    off_ys = A.mark()
    ysb = A.alloc([128, 4, OWN], BF16)
    ysw = A.alloc([128, 4, OWN], BF16)
    mQT = A.mark()
    QT = A.alloc([128, 4, OWN], BF16)

    sh_a = modT[:, 0:8]
    onep_a = onep[:, 8:16]
    onep_ga = onep[:, 16:24]
    sh_f = modT[:, 24:32]
    onep_f = onep[:, 32:40]
    onep_gf = onep[:, 40:48]

    def MM(out, lhsT, rhs, start, stop):
        return lambda e: e.matmul(out, lhsT=lhsT, rhs=rhs, start=start, stop=stop, skip_group_check=True)

    def TR(out, in_):
        return lambda e: e.transpose(out, in_, ident)

    def ACTF(out, in_, func, bias=None, scale=None):
        kw = {}
        if bias is not None:
            kw["bias"] = bias
        if scale is not None:
            kw["scale"] = scale
        return lambda e: e.activation(out=out, in_=in_, func=func, **kw)

    def TT(out, in0, in1, op):
        return lambda e: e.tensor_tensor(out=out, in0=in0, in1=in1, op=op)

    def TS(out, in0, s1, s2, op0, op1=None):
        if op1 is None:
            return lambda e: e.tensor_scalar(out=out, in0=in0, scalar1=s1, scalar2=None, op0=op0)
        return lambda e: e.tensor_scalar(out=out, in0=in0, scalar1=s1, scalar2=s2, op0=op0, op1=op1)

    def STT(out, in0, scalar, in1, op0, op1):
        return lambda e: e.scalar_tensor_tensor(out=out, in0=in0, scalar=scalar, in1=in1, op0=op0, op1=op1)

    def COPY(out, in_):
        return lambda e: e.tensor_copy(out=out, in_=in_)

    def RED(out, in_, op):
        return lambda e: e.tensor_reduce(out=out, in_=in_, axis=AX.X, op=op)

    def RECIP(out, in_):
        return lambda e: e.reciprocal(out=out, in_=in_)

    def MEMSET(out, v):
        return lambda e: e.memset(out, v)

    def DMA(out, in_):
        return lambda e: e.dma_start(out=out, in_=in_)

    def dma(out, in_, reads, writes):
        return P.add("sp", DMA(out, in_), reads=reads, writes=writes, dma=True)

    dma(cst, cst_d[:, :, :], [], ["cst"])
    dma(Mk, mk_d[:, :, :], [], ["mk"])
    dma(vec, vec_d[:, :], [], ["vec"])
    dma(bvb, bvb_d[:, :], [], ["bvb"])

    cv_rr = [0]

    def load_weight(dst, src_view, ncols, key, dup=False):
        for c0 in range(0, ncols, 256):
            w = min(256, ncols - c0)
            sl = cv_rr[0] % 2
            st = wstage[sl]
            dma(st[:, :, 0:w], src_view[:, :, c0:c0 + w], [], [("wst", sl)])
            eng = "dve" if (cv_rr[0] % 2 == 0) else "pool"
            cv_rr[0] += 1
            if not dup:
                P.add(eng, COPY(dst[:, :, c0:c0 + w], st[:, :, 0:w]), [("wst", sl)], [key])
            else:
                for g in range(w // 64):
                    for d2 in range(2):
                        P.add(eng, COPY(dst[:, :, g * 128 + d2 * 64: g * 128 + d2 * 64 + 64], st[:, :, g * 64:(g + 1) * 64]),
                              [("wst", sl)], [key])

    mod_rr = [0]

    def modulate(xs_t, xb_t, skey, bkey, onep_v, sh_v):
        for kc in range(8):
            eng = "dve"
            mod_rr[0] += 1
            P.add(eng, TS(xb_t[:, kc, :], xs_t[:, kc, :], onep_v[:, kc:kc + 1], sh_v[:, kc:kc + 1], ALU.mult, ALU.add),
                  [skey, "modA"], [(bkey, kc)])

    bank_rr = [0]

    nbanks = [7]

    def next_bank():
        b = bank_rr[0] % nbanks[0]
        bank_rr[0] += 1
        return b

    w_in_v = w_in.rearrange("(kc p) n -> p kc n", p=128)
    xq_v = xq.rearrange("(kc p) t -> p kc t", p=128)
    xh_v = xh.rearrange("(kc p) t -> p kc t", p=128)
    xT_v = xT.rearrange("(kc p) t -> p kc t", p=128)
    chunk_id = [0]

    def stage_chunk(xs_l, xb_l, src):
        i = chunk_id[0]
        chunk_id[0] += 1
        sl = i % 2
        dma(xs_l[sl], src, [], [("xs", sl)])
        modulate(xs_l[sl], xb_l[sl], ("xs", sl), ("xb", sl), onep_a, sh_a)
        return xb_l[sl], [(("xb", sl), kc) for kc in range(8)]

    def proj_fm(xb_t, xkeys, W, wkey, c0):
        b = next_bank()
        for kc in range(8):
            P.add("pe", MM(bank(b), W[:, kc, c0:c0 + 128], xb_t[:, kc, :], kc == 0, kc == 7), [wkey, xkeys[kc]], [("bank", b)])
        return b

    def proj_tm(xb_t, xkeys, W, wkey, i, ncols):
        b = next_bank()
        for kc in range(8):
            P.add("pe", MM(bank(b)[:, 0:ncols], xb_t[:, kc, i * 128:(i + 1) * 128], W[:, kc, 0:ncols], kc == 0, kc == 7),
                  [wkey, xkeys[kc]], [("bank", b)])
        return b

    mA = A.mark()
    Wq = A.alloc([128, 8, 512], BF16)
    Wqs = A.alloc([128, 8, 512], BF16)
    Wks = A.alloc([128, 8, 256], BF16)
    Wvs = A.alloc([128, 8, 256], BF16)
    qsT = A.alloc([128, 4, OWN], BF16)
    kcat = A.alloc([128, 2, 16, 256], BF16)
    vsw = A.alloc([128, 16, 2, 256], BF16)
    m0 = A.mark()
    wa_stage = [A.alloc([128, 8, 512], F32) for _ in range(2)]
    wab0 = [A.alloc([128, 8, 512], BF16) for _ in range(2)]
    silu_bf = A_small_bf
    P.add("act", ACTF(tmp8, c_c, AF.Exp, scale=-1.0), ["vec"], ["tmp8"])
    P.add("dve", TS(tmp8, tmp8, 1.0, None, ALU.add), ["tmp8"], ["tmp8"])
    P.add("dve", RECIP(tmp8, tmp8), ["tmp8"], ["tmp8"])
    P.add("dve", TT(silu_c, c_c, tmp8, ALU.mult), ["tmp8", "vec"], ["silu"])
    P.add("dve", COPY(silu_bf, silu_c), ["silu"], ["silubf"])
    w_ada_v = w_ada.rearrange("(kc p) n -> p kc n", p=128)
    modp = bank(7)[:, 0:48]

    def mod_cols(stb, skey, col, c_in_piece):
        for kc in range(8):
            P.add("pe", MM(modp[:, col:col + 1], stb[:, kc, c_in_piece * 128:(c_in_piece + 1) * 128], silu_bf[:, kc:kc + 1], kc == 0, kc == 7),
                  [skey, "silubf"], [("modp", col)])

    for pc in range(4):
        st = wa_stage[pc % 2]
        dma(st, w_ada_v[:, :, pc * 512:(pc + 1) * 512], [], [("wa", pc % 2)])
        if pc == 1:
            load_weight(Wq, w_in_v[:, :, 0:512], 512, "Wq")
            load_weight(Wqs, w_in_v[:, :, 1536:2048], 512, "Wqs")
            load_weight(Wks, w_in_v[:, :, 2048:2176], 128, "Wks", dup=True)
            load_weight(Wvs, w_in_v[:, :, 2176:2304], 128, "Wvs", dup=True)
        P.add("dve", COPY(wab0[pc % 2], st), [("wa", pc % 2)], [("wab0", pc % 2)])
        for cc in range(4):
            mod_cols(wab0[pc % 2], ("wab0", pc % 2), pc * 4 + cc, cc)
    P.add("dve", TT(modT[:, 0:16], modp[:, 0:16], b_ada_c[:, 0:16], ALU.add), [("modp", c) for c in range(16)] + ["vec"], ["modA"])
    P.add("dve", TS(onep[:, 0:16], modT[:, 0:16], 1.0, None, ALU.add), ["modA"], ["modA"])
    P.add("dve", TS(bq8, b_in_c, 0.125, None, ALU.mult), ["vec"], ["bq8"])
    P.barrier()
    A.release(m0)
    if stop == "mod":
        return dbg_stop([(modT[:, 0:16], 16), (onep[:, 0:16], 16), (silu_c, 8)])

    mA2 = A.mark()
    xsA = [A.alloc([128, 8, 512], F32) for _ in range(2)]
    xbA = [A.alloc([128, 8, 512], BF16) for _ in range(2)]
    bk_sw = bvb[:, 512:768]

    for t in range(4):
        xb_t, xk = stage_chunk(xsA, xbA, xq_v[:, :, t * 512:(t + 1) * 512])
        for hp in range(4):
            b = proj_fm(xb_t, xk, Wq, "Wq", hp * 128)
            P.add("act", ACTF(QT[:, hp, t * 512:(t + 1) * 512], bank(b), AF.Identity, bias=bq8[:, hp:hp + 1], scale=0.125),
                  [("bank", b), "bq8"], [("QT", hp, t)])
        for hp in range(4):
            b = proj_fm(xb_t, xk, Wqs, "Wqs", hp * 128)
            P.add("act", ACTF(qsT[:, hp, t * 512:(t + 1) * 512], bank(b), AF.Identity, bias=b_in_c[:, 12 + hp:13 + hp], scale=1.0),
                  [("bank", b), "vec"], [("qsT", hp, t)])
        for g in range(2):
            b = proj_fm(xb_t, xk, Wks, "Wks", g * 128)
            P.add("dve", TS(kcat[:, g, 4 * t:4 * t + 4, 128:256], bank(b).rearrange("p (a b) -> p a b", a=4),
                            vec[:, 122 + g:123 + g], None, ALU.add), [("bank", b), "vec"], [("kcat", g, t, 1)])
        for i in range(4):
            b = proj_tm(xb_t, xk, Wvs, "Wvs", i, 256)
            P.add("dve", TT(vsw[:, 4 * t + i, 1, :], bank(b)[:, 0:256], bk_sw, ALU.add), [("bank", b), "bvb"], [("vsw", 4 * t + i, 1)])
    for t in range(4):
        xb_t, xk = stage_chunk(xsA, xbA, xh_v[:, :, t * 512:(t + 1) * 512])
        for g in range(2):
            b = proj_fm(xb_t, xk, Wks, "Wks", g * 128)
            P.add("dve", TS(kcat[:, g, 4 * t:4 * t + 4, 0:128], bank(b).rearrange("p (a b) -> p a b", a=4),
                            vec[:, 122 + g:123 + g], None, ALU.add), [("bank", b), "vec"], [("kcat", g, t, 0)])
        for i in range(4):
            b = proj_tm(xb_t, xk, Wvs, "Wvs", i, 256)
            P.add("dve", TT(vsw[:, 4 * t + i, 0, :], bank(b)[:, 0:256], bk_sw, ALU.add), [("bank", b), "bvb"], [("vsw", 4 * t + i, 0)])
    P.barrier()
    A.release(mA2)
    if stop == "projA":
        return dbg_stop([(modT, 48)])

    swab = A.alloc([128, 2, 8, 256], F32)
    dma(swab, swab_d[:, :, :, :], [], ["swab"])
    if stop == "swa0a":
        return dbg_stop([(swab[:, 0, 0, :], 256)])
    Ssb = [A.alloc([128, 4, 256], F32) for _ in range(2)]
    pn = [A.alloc([128, 4, 256], BF16) for _ in range(2)]
    pT = [A.alloc([128, 8, 128], BF16) for _ in range(2)]
    st4 = [A.alloc([128, 16], F32) for _ in range(2)]
    wab = [A.alloc([128, 8, 256], BF16) for _ in range(2)]
    def ACTF_ACC(out, in_, func, bias, acc):
        return lambda e: e.activation(out=out, in_=in_, func=func, bias=bias, scale=1.0, accum_out=acc)

    def swa_ctx(it):
        m, g = it // 2, it % 2
        sl = it % 2
        bS = (0, 1) if sl == 0 else (4, 5)
        bT = 2
        bO = 3 if sl == 0 else 6
        Sps = ps[:, bS[0] * 512:bS[0] * 512 + 1024].rearrange("p (a b) -> p a b", a=4)
        return m, g, sl, bS, bT, bO, Sps

    def swa_S(it):
        m, g, sl, bS, bT, bO, Sps = swa_ctx(it)
        for hh in range(4):
            h = 4 * g + hh
            pair, half = h // 2, h % 2
            a = half * 2 + hh // 2
            P.add("pe", MM(Sps[:, a, :], qsT[half * 64:(half + 1) * 64, pair, m * 128:(m + 1) * 128],
                           kcat[half * 64:(half + 1) * 64, g, m, :], True, True),
                  [("qsT", pair, m // 4), ("kcat", g, m // 4, 0), ("kcat", g, m // 4, 1)], [("bank", bS[half])])

    def swa_front(it):
        m, g, sl, bS, bT, bO, Sps = swa_ctx(it)
        Sv, sv = Ssb[sl], st4[sl]
        mx = sv[:, 0:4]
        rs = sv[:, 4:8]
        es = sv[:, 8:12]
        dd = sv[:, 12:16]
        skey = ("Ssb", sl)
        stk = ("st", sl)
        bsel = 0 if m == 0 else 1
        P.add("dve", STT(Sv, Sps, 0.125, swab[:, bsel, 4 * g:4 * g + 4, :], ALU.mult, ALU.add),
              [("bank", bS[0]), ("bank", bS[1]), "swab"], [skey])
        P.add("dve", RED(mx, Sv, ALU.max), [skey], [stk])
        P.add("dve", TT(mx, mx, sinks_c[:, 4 * g:4 * g + 4], ALU.max), [stk, "vec"], [stk])
        P.add("dve", TS(mx, mx, -1.0, None, ALU.mult), [stk], [stk])
        P.add("dve", TT(dd, sinks_c[:, 4 * g:4 * g + 4], mx, ALU.add), [stk, "vec"], [stk])
        for a in range(4):
            P.add("act", ACTF_ACC(Sv[:, a, :], Sv[:, a, :], AF.Exp, mx[:, a:a + 1], rs[:, a:a + 1]), [skey, stk], [skey, ("rs", sl)])
        P.add("act", ACTF(es, dd, AF.Exp), [stk, ("rs", sl)], [("es", sl)])

    def swa_back(it):
        m, g, sl, bS, bT, bO, Sps = swa_ctx(it)
        Sv, pnv, sv = Ssb[sl], pn[sl], st4[sl]
        rs = sv[:, 4:8]
        es = sv[:, 8:12]
        skey = ("Ssb", sl)
        P.add("dve", TT(rs, rs, es, ALU.add), [("es", sl), ("rs", sl)], [("rs", sl)])
        P.add("dve", RECIP(rs, rs), [("rs", sl)], [("rs", sl)])
        P.add("dve", TT(pnv, Sv, rs.unsqueeze(2).to_broadcast([128, 4, 256]), ALU.mult), [skey, ("rs", sl)], [("pn", sl)])

    def swa_tr(it):
        m, g, sl, bS, bT, bO, Sps = swa_ctx(it)
        pnv, pTv = pn[sl], pT[sl]
        Tps = bank(bT).bitcast(BF16)
        for a in range(4):
            for blk in range(2):
                idx = a * 2 + blk
                P.add("pe", TR(Tps[:, idx * 128:(idx + 1) * 128], pnv[:, a, blk * 128:(blk + 1) * 128]),
                      [("pn", sl), "cst"], [("bank", bT)])
        P.add("act", ACTF(pTv, Tps.rearrange("p (a b) -> p a b", a=8), AF.Copy), [("bank", bT)], [("pT", sl)])

    def swa_pv(it):
        m, g, sl, bS, bT, bO, Sps = swa_ctx(it)
        pTv = pT[sl]
        Ops = bank(bO)
        for a in range(4):
            for blk in range(2):
                P.add("pe", MM(Ops[:, a * 128:(a + 1) * 128], vsw[:, m, blk, g * 128:(g + 1) * 128], pTv[:, a * 2 + blk, :],
                               blk == 0, blk == 1), [("pT", sl), ("vsw", m, blk)], [("bank", bO)])
        Ov = Ops.rearrange("p (h t) -> p h t", h=4)
        P.add("act", ACTF(ysw[0:64, 2 * g:2 * g + 2, m * 128:(m + 1) * 128], Ov[0:64, 0:2, :], AF.Copy),
              [("bank", bO)], [("ysw", m, g, 0)])
        P.add("act", ACTF(ysw[64:128, 2 * g:2 * g + 2, m * 128:(m + 1) * 128], Ov[64:128, 2:4, :], AF.Copy),
              [("bank", bO)], [("ysw", m, g, 1)])

    swa_S(0)
    swa_front(0)
    swa_S(1)
    for it in range(32):
        if it + 1 < 32:
            swa_front(it + 1)
        swa_back(it)
        if it + 2 < 32:
            swa_S(it + 2)
        col = 16 + it
        wsl = (it // 2) % 2
        if it % 2 == 0:
            dma(wstage[wsl], w_ada_v[:, :, col * 128:col * 128 + 256], [], [("wst", wsl)])
            P.add("dve", COPY(wab[wsl], wstage[wsl]), [("wst", wsl)], [("wab", wsl)])
        mod_cols(wab[wsl], ("wab", wsl), col, it % 2)
        swa_tr(it)
        swa_pv(it)
    P.add("dve", TT(modT[:, 16:48], modp[:, 16:48], b_ada_c[:, 16:48], ALU.add), [("modp", c) for c in range(16, 48)] + ["vec"], ["modB"])
    P.add("dve", TS(onep[:, 16:48], modT[:, 16:48], 1.0, None, ALU.add), ["modB"], ["modB"])
    P.barrier()
    A.release(mA)
    nbanks[0] = 8
    if stop == "swa":
        return dbg_stop([(modT, 48)])

    for hf in range(2):
        mB = A.mark()
        KT = A.alloc([128, 2, S], BF16)
        Vt = A.alloc([128, NB, 256], BF16)
        mB2 = A.mark()
        Wk = A.alloc([128, 8, 256], BF16)
        Wv = A.alloc([128, 8, 256], BF16)
        xsB = [A.alloc([128, 8, 512], F32) for _ in range(2)]
        xbB = [A.alloc([128, 8, 512], BF16) for _ in range(2)]
        load_weight(Wk, w_in_v[:, :, 512 + hf * 256:512 + (hf + 1) * 256], 256, "Wk")
        load_weight(Wv, w_in_v[:, :, 1024 + hf * 256:1024 + (hf + 1) * 256], 256, "Wv")
        bv_sb = bvb[:, hf * 256:(hf + 1) * 256]
        for tc in range(16):
            xb_t, xk = stage_chunk(xsB, xbB, xT_v[:, :, tc * 512:(tc + 1) * 512])
            for lp in range(2):
                b = proj_fm(xb_t, xk, Wk, "Wk", lp * 128)
                P.add("act", ACTF(KT[:, lp, tc * 512:(tc + 1) * 512], bank(b), AF.Identity,
                                  bias=b_in_c[:, 4 + 2 * hf + lp:5 + 2 * hf + lp], scale=1.0), [("bank", b), "vec"], [("KT", lp, tc)])
            for i in range(4):
                b = proj_tm(xb_t, xk, Wv, "Wv", i, 256)
                P.add("dve", TT(Vt[:, 4 * tc + i, :], bank(b)[:, 0:256], bv_sb, ALU.add), [("bank", b), "bvb"], [("Vt", tc)])
        P.barrier()
        A.release(mB2)
        if stop == "projB":
            return dbg_stop([(modT, 48)])

        Esb = [A.alloc([128, 2, 512], F32) for _ in range(2)]
        Lsb = [A.alloc([128, 2, 512], BF16) for _ in range(2)]
        Tsb = [A.alloc([128, 2, 512], F32) for _ in range(3)]
        Psb = [A.alloc([128, 2, 512], BF16) for _ in range(2)]
        Rneg = [A.alloc([128, 2, 512], F32) for _ in range(2)]
        Ybk = [ps[:, 0:1024].rearrange("p (a b) -> p a b", a=2), ps[:, 1024:2048].rearrange("p (a b) -> p a b", a=2)]
        RB = ps[:, 2048:3072].rearrange("p (a b) -> p a b", a=2)
        Oab = [bank(6), bank(7)]

        def sb_stream(lp, hp, M, KT=KT, Vt=Vt, Esb=Esb, Lsb=Lsb, Tsb=Tsb, Psb=Psb, Rneg=Rneg):
            n = 16 * M + 16
            J = 16 * M + 15

            def info(k):
                j = J - k
                p = j - 16 * M
                q0 = (p // 4) if p >= 0 else 0
                return j, p, q0 * 128

            def emit_Z(k):
                j, p, c0 = info(k)
                s = k % 2
                for hd in range(2):
                    P.add("pe", MM(Ybk[s][:, hd, c0:512], KT[hd * 64:(hd + 1) * 64, lp, j * 128:(j + 1) * 128],
                                   QT[hd * 64:(hd + 1) * 64, hp, M * 512 + c0:M * 512 + 512], True, False),
                          [("KT", lp, j // 4), ("QT", hp, M)], [("Y", s)])
                if p >= 0:
                    q, ee = p // 4, p % 4
                    for hd in range(2):
                        P.add("pe", MM(Ybk[s][:, hd, q * 128:(q + 1) * 128], ident, Mk[:, ee, :], False, False),
                              ["cst", "mk"], [("Y", s)])

            def emit_E(k):
                j, p, c0 = info(k)
                s = k % 2
                P.add("act", ACTF(Esb[s][:, :, c0:512], Ybk[s][:, :, c0:512], AF.Exp), [("Y", s)], [("E", s)])

            def emit_L(k):
                j, p, c0 = info(k)
                s = k % 2
                P.add("act", ACTF(Lsb[s][:, :, c0:512], Esb[s][:, :, c0:512], AF.Ln, bias=1.0, scale=1.0), [("E", s)], [("L", s)])

            def emit_UO(k):
                j, p, c0 = info(k)
                s = k % 2
                for hd in range(2):
                    P.add("pe", MM(Ybk[s][:, hd, c0:512], Umat, Lsb[s][:, hd, c0:512], False, True), ["cst", ("L", s)], [("Y", s)])
                for hd in range(2):
                    P.add("pe", MM(RB[:, hd, c0:512], negones, Lsb[s][:, hd, c0:512], True, True), ["cst", ("L", s)], ["RB"])

            def emit_T(k):
                j, p, c0 = info(k)
                s = k % 2
                P.add("dve", TT(Tsb[k % 3][:, :, c0:512], Ybk[s][:, :, c0:512], Rneg[k % 2][:, :, c0:512], ALU.add),
                      [("Y", s), ("R", k % 2)], [("T", k % 3)])
                P.add("dve", TT(Rneg[(k + 1) % 2][:, :, c0:512], RB[:, :, c0:512], Rneg[k % 2][:, :, c0:512], ALU.add),
                      ["RB", ("R", k % 2)], [("R", (k + 1) % 2)])

            def emit_P(k):
                j, p, c0 = info(k)
                s = k % 2
                P.add("act", ACTF(Psb[s][:, :, c0:512], Tsb[k % 3][:, :, c0:512], AF.Exp), [("T", k % 3)], [("P", s)])

            def emit_PV(k):
                j, p, c0 = info(k)
                s = k % 2
                for hd in range(2):
                    P.add("pe", MM(Oab[hd][:, c0:512], Vt[:, j, lp * 128:(lp + 1) * 128], Psb[s][:, hd, c0:512], k == 0, k == n - 1),
                          [("Vt", j // 4), ("P", s)], [("O", hd)])

            P.add("pool", MEMSET(Rneg[0], 0.0), [], [("R", 0)])
            P.add("pool", MEMSET(Rneg[1], 0.0), [], [("R", 1)])
            emit_Z(0)
            for k in range(n + 2):
                if k < n:
                    emit_E(k)
                if k + 1 < n:
                    emit_Z(k + 1)
                if k - 2 >= 0:
                    emit_P(k - 2)
                    emit_PV(k - 2)
                if k < n:
                    emit_L(k)
                    emit_UO(k)
                    emit_T(k)
            P.add("dve", COPY(ysb[0:64, hp, M * 512:(M + 1) * 512], Oab[0][0:64, :]), [("O", 0)], [("ysb", hp, M, 0)])
            P.add("dve", COPY(ysb[64:128, hp, M * 512:(M + 1) * 512], Oab[1][64:128, :]), [("O", 1)], [("ysb", hp, M, 1)])

        for lp in range(2):
            for M in range(4):
                sb_stream(lp, 2 * hf + lp, M)
        P.barrier()
        A.release(mB)
        if stop == "sb0":
            return dbg_stop([(modT, 48)])

    A.release(mQT)
    x1T = A.alloc([128, 8, OWN], F32)
    mX = A.mark()
    Wo = A.alloc([128, 8, 1024], BF16)
    load_weight(Wo, w_out.rearrange("(kc p) n -> p kc n", p=128), 1024, "Wo")
    xres = A.alloc([128, 8, 512], F32)
    mixed = A.alloc([128, 8, 512], BF16)
    sqbC = A.alloc([128, 8, 512], BF16)
    ubfC = A.alloc([128, 8, 512], BF16)
    sttC = [A.alloc([128, 512], F32) for _ in range(4)]

    def layer_norm_tile(tok0, gcols, bcols, ubf, sqb, stt):
        uv = x1T[:, :, tok0:tok0 + 512]
        ukeys = [("x1", tok0 // 512, c) for c in range(8)]
        P.add("dve", COPY(ubf, uv), ukeys, ["ubf"])
        P.add("act", ACTF(sqb, uv, AF.Square), ukeys, ["sqb"])
        bm, bq = next_bank(), next_bank()
        for c in range(8):
            P.add("pe", MM(bank(bm), onesM, ubf[:, c, :], c == 0, c == 7), ["cst", "ubf"], [("bank", bm)])
        for c in range(8):
            P.add("pe", MM(bank(bq), onesM, sqb[:, c, :], c == 0, c == 7), ["cst", "sqb"], [("bank", bq)])
        m2, var, rstd, nmr = stt
        P.add("act", ACTF(m2, bank(bm), AF.Square), [("bank", bm)], ["m2"])
        P.add("dve", TT(var, bank(bq), m2, ALU.subtract), [("bank", bq), "m2"], ["var"])
        P.add("act", ACTF(var, var, AF.Ln, bias=1e-5, scale=1.0), ["var"], ["var"])
        P.add("act", ACTF(rstd, var, AF.Exp, scale=-0.5), ["var"], ["rstd"])
        P.add("dve", STT(nmr, bank(bm), -1.0, rstd, ALU.mult, ALU.mult), [("bank", bm), "rstd"], ["nmr"])
        for c in range(8):
            eng = "dve"
            ukey = ukeys[c]
            P.add(eng, TT(uv[:, c, :], uv[:, c, :], rstd, ALU.mult), [ukey, "rstd"], [ukey])
            P.add(eng, TT(uv[:, c, :], uv[:, c, :], nmr, ALU.add), [ukey, "nmr"], [ukey])
            P.add("act", ACTF(uv[:, c, :], uv[:, c, :], AF.Identity, bias=bcols[:, c:c + 1], scale=gcols[:, c:c + 1]), [ukey, "vec"], [ukey])

    for tt in range(4):
        tok0 = tt * 512
        dma(xres, xq_v[:, :, tok0:tok0 + 512], [], ["xres"])
        for G, yv in ((0, ysb), (1, ysw)):
            yt = yv[:, :, tok0:tok0 + 512]
            P.add("dve", TT(sqbC[:, 4 * G:4 * G + 4, :], yt, yt, ALU.mult), [], [("sq", G)])
            bms = next_bank()
            for c in range(4):
                P.add("pe", MM(bank(bms), ones512, sqbC[:, 4 * G + c, :], c == 0, c == 3), ["cst", ("sq", G)], [("bank", bms)])
            rt = sttC[G]
            P.add("act", ACTF(rt, bank(bms), AF.Ln, bias=1e-6, scale=1.0), [("bank", bms)], [("rt", G)])
            P.add("act", ACTF(rt, rt, AF.Exp, scale=-0.5), [("rt", G)], [("rt", G)])
            for c in range(4):
                P.add("dve", STT(mixed[:, 4 * G + c, :], yt[:, c, :], gn_c[:, 4 * G + c:4 * G + c + 1], rt, ALU.mult, ALU.mult),
                      [("rt", G), "vec"], [("mixed", 4 * G + c)])
        P.add("act", ACTF(xres, xres, AF.Identity, scale=ALPHA), ["xres"], ["xres"])
        for co in range(8):
            b = next_bank()
            for c in range(8):
                P.add("pe", MM(bank(b), Wo[:, c, co * 128:(co + 1) * 128], mixed[:, c, :], c == 0, c == 7), ["Wo", ("mixed", c)], [("bank", b)])
            P.add("dve", STT(x1T[:, co, tok0:tok0 + 512], bank(b), onep_ga[:, co:co + 1], xres[:, co, :], ALU.mult, ALU.add),
                  [("bank", b), "xres", "modB"], [("x1", tt, co)])
        layer_norm_tile(tok0, ln1g, ln1b, ubfC, sqbC, sttC)
    P.barrier()
    A.release(mX)
    if stop == "phC":
        return dbg_stop([(x1T[:, 0, :], 2048)])

    sqbD = A.alloc([128, 8, 512], BF16)
    ubfD = A.alloc([128, 8, 512], BF16)
    sttD = [A.alloc([128, 512], F32) for _ in range(4)]
    actT = A.alloc([128, NFC, 1024], BF16)
    Wd = [A.alloc([128, NFC, 128], BF16) for _ in range(2)]
    A2 = Arena(A.ap[:, off_ys // 4:(off_ys + 32768) // 4], 32768)
    h2s = A2.alloc([128, 2, 8, 512], BF16)
    Wgu = [A2.alloc([128, 8, 256], BF16) for _ in range(2)]
    sg = [A2.alloc([128, 512], F32) for _ in range(2)]
    w_gu_v = w_gu.rearrange("(kc p) n -> p kc n", p=128)
    w_down_v = w_down.rearrange("(fc p) n -> p fc n", p=128)
    outT_v = outT.rearrange("(kc p) t -> p kc t", p=128)
    wg_i = 0
    wd_i = 0
    sg_i = 0
    for TT_ in range(2):
        for sub in range(2):
            tok0 = TT_ * 1024 + sub * 512
            for c in range(8):
                eng = "dve" if c % 2 == 0 else "pool"
                P.add(eng, TS(h2s[:, sub, c, :], x1T[:, c, tok0:tok0 + 512], onep_f[:, c:c + 1], sh_f[:, c:c + 1], ALU.mult, ALU.add),
                      [("x1", tok0 // 512, c), "modB"], [("h2", sub, c)])
        for fc in range(NFC):
            sl = wg_i % 2
            wg_i += 1
            wsl = cv_rr[0] % 2
            cv_rr[0] += 1
            st = wstage[wsl]
            skey = ("wst", wsl)
            dma(st[:, :, 0:128], w_gu_v[:, :, fc * 128:(fc + 1) * 128], [], [skey])
            dma(st[:, :, 128:256], w_gu_v[:, :, DFF + fc * 128:DFF + (fc + 1) * 128], [], [skey])
            P.add("pool", COPY(Wgu[sl], st), [skey], [("Wgu", sl)])
            for sub in range(2):
                bg, bu = next_bank(), next_bank()
                for kc in range(8):
                    P.add("pe", MM(bank(bg), Wgu[sl][:, kc, 0:128], h2s[:, sub, kc, :], kc == 0, kc == 7),
                          [("Wgu", sl), ("h2", sub, kc)], [("bank", bg)])
                for kc in range(8):
                    P.add("pe", MM(bank(bu), Wgu[sl][:, kc, 128:256], h2s[:, sub, kc, :], kc == 0, kc == 7),
                          [("Wgu", sl), ("h2", sub, kc)], [("bank", bu)])
                ss = sg_i % 2
                sg_i += 1
                P.add("act", ACTF(sg[ss], bank(bg), AF.Silu), [("bank", bg)], [("sg", ss)])
                P.add("dve", TT(actT[:, fc, sub * 512:(sub + 1) * 512], sg[ss], bank(bu), ALU.mult), [("sg", ss), ("bank", bu)], [("act", fc, sub)])
        for sub in range(2):
            tok0 = TT_ * 1024 + sub * 512
            P.add("act", ACTF(x1T[:, :, tok0:tok0 + 512], x1T[:, :, tok0:tok0 + 512], AF.Identity, scale=ALPHA),
                  [("x1", tok0 // 512, c) for c in range(8)] + [("h2", sub, c) for c in range(8)],
                  [("x1", tok0 // 512, c) for c in range(8)])
        for co in range(8):
            sl = wd_i % 2
            wd_i += 1
            for h2_ in range(2):
                wsl = cv_rr[0] % 2
                cv_rr[0] += 1
                stv = wstage_flat[wsl][:, 0:11 * 128].rearrange("p (a b) -> p a b", a=11)
                dma(stv, w_down_v[:, h2_ * 11:(h2_ + 1) * 11, co * 128:(co + 1) * 128], [], [("wst", wsl)])
                P.add("pool" if h2_ == 0 else "dve", COPY(Wd[sl][:, h2_ * 11:(h2_ + 1) * 11, :], stv), [("wst", wsl)], [("Wd", sl, h2_)])
            for sub in range(2):
                tok0 = TT_ * 1024 + sub * 512
                b = next_bank()
                for fc in range(NFC):
                    P.add("pe", MM(bank(b), Wd[sl][:, fc, :], actT[:, fc, sub * 512:(sub + 1) * 512], fc == 0, fc == NFC - 1),
                          [("Wd", sl, fc // 11), ("act", fc, sub)], [("bank", b)])
                P.add("dve", STT(x1T[:, co, tok0:tok0 + 512], bank(b), onep_gf[:, co:co + 1], x1T[:, co, tok0:tok0 + 512], ALU.mult, ALU.add),
                      [("bank", b), ("x1", tok0 // 512, co), "modB"], [("x1", tok0 // 512, co)])
        for sub in range(2):
            tok0 = TT_ * 1024 + sub * 512
            layer_norm_tile(tok0, ln2g, ln2b, ubfD, sqbD, sttD)
            P.add("sp", DMA(outT_v[:, :, tok0:tok0 + 512], x1T[:, :, tok0:tok0 + 512]),
                  reads=[("x1", tok0 // 512, c) for c in range(8)], writes=[("out", tok0)], dma=True)
    P.add("sp", None, reads=[("out", t * 512) for t in range(4)], writes=[])
    print("arena peak bytes", A.peak, "n_ops", len(P.ops))
    return finish()


_CACHE = {}


def _consts():
    bf = ml_dtypes.bfloat16
    cst = np.zeros((128, 5, 128), np.float32)
    cst[:, 0, :] = np.eye(128)
    sp, s_ = np.meshgrid(np.arange(128), np.arange(128), indexing="ij")
    cst[:, 1, :] = -(sp >= s_).astype(np.float32)
    cst[:, 2, :] = -1.0
    cst[:, 3, :] = 1.0 / 1024.0
    cst[:, 4, :] = 1.0 / 512.0
    return cst.astype(bf)


def kernel(x, c, w_ada, b_ada, w_in, b_in, sinks, gn_sb, gn_swa, w_out,
           ln1_g, ln1_b, w_gu, w_down, ln2_g, ln2_b):
    bf = ml_dtypes.bfloat16
    x = np.asarray(x, np.float32)
    f = lambda a: np.ascontiguousarray(np.asarray(a, np.float32))
    if "nc" not in _CACHE:
        _CACHE["nc"] = build_program()
    nc, P, A = _CACHE["nc"]

    cst = _consts()
    col = lambda v: np.asarray(v, np.float32).reshape(-1, 128).T
    b_in0 = np.asarray(b_in[0], np.float32)
    shared = {
        "w_ada": f(w_ada[0]), "w_in": f(w_in[0]), "w_out": f(w_out[0]), "w_gu": f(w_gu[0]), "w_down": f(w_down[0]),
        "cst": cst,
    }
    bvb = np.zeros((128, 768), np.float32)
    bvb[:, 0:512] = b_in0[1024:1536][None, :]
    bvsw = b_in0[2176:2304].reshape(2, 64)
    bvb[:, 512:768] = np.concatenate([bvsw[0], bvsw[0], bvsw[1], bvsw[1]])[None, :]
    shared["bvb"] = bvb
    t_i = np.arange(128)[:, None]
    s_i = np.arange(256)[None, :]
    dist = (t_i + 128 - s_i).astype(np.float32)
    inband = (dist >= 0) & (dist < 128)
    slopes = np.exp2(-8.0 * np.arange(1, 9, dtype=np.float32) / 8.0)
    base = np.where(inband[None], -slopes[:, None, None] * dist[None], NEG_BIG).astype(np.float32)
    base = np.transpose(base, (1, 0, 2))
    HPERM = np.array([0, 2, 1, 3, 4, 6, 5, 7])
    base = np.ascontiguousarray(base[:, HPERM, :])
    nohalo = base.copy()
    nohalo[:, :, 0:128] = NEG_BIG

    in_maps = []
    for core in range(8):
        b, r = core // 4, core % 4
        xb_ = x[b]
        blocks = xb_.reshape(NB, 128, D)
        own = blocks[r::4]
        halo_idx = np.arange(16) * 4 + r - 1
        halo = np.zeros_like(own)
        valid = halo_idx >= 0
        halo[valid] = blocks[halo_idx[valid]]
        vecs = np.zeros((128, 128), np.float32)
        vecs[:, 0:18] = col(b_in0)
        vecs[:, 18:22] = col(gn_sb[0])
        vecs[:, 22:26] = col(gn_swa[0])
        vecs[:, 26:34] = col(ln1_g[0])
        vecs[:, 34:42] = col(ln1_b[0])
        vecs[:, 42:50] = col(ln2_g[0])
        vecs[:, 50:58] = col(ln2_b[0])
        vecs[:, 58:66] = np.asarray(sinks[0], np.float32)[HPERM][None, :]
        vecs[:, 66:114] = col(b_ada[0])
        vecs[:, 114:122] = col(c[b])
        bks = b_in0[2048:2176].reshape(2, 64)
        vecs[:, 122] = np.concatenate([bks[0], bks[0]])
        vecs[:, 123] = np.concatenate([bks[1], bks[1]])
        mk = np.zeros((128, 4, 128), np.float32)
        ss, tt = np.meshgrid(np.arange(128), np.arange(128), indexing="ij")
        for e in range(4):
            if e == r:
                mk[:, e, :] = np.where(ss >= tt, MASKV, 0.0)
            elif e > r:
                mk[:, e, :] = MASKV
        swab = np.stack([nohalo if r == 0 else base, base], axis=1)
        m = dict(shared)
        m.update({
            "xT": np.ascontiguousarray(xb_.T),
            "xq": np.ascontiguousarray(own.reshape(OWN, D).T),
            "xh": np.ascontiguousarray(halo.reshape(OWN, D).T),
            "mk": mk.astype(bf),
            "vec": vecs,
            "swab": np.ascontiguousarray(swab),
        })
        in_maps.append(m)

    if _CACHE.get("early"):
        for m_ in in_maps:
            for k_ in ("xT", "w_out", "w_gu", "w_down"):
                m_[k_] = np.ascontiguousarray(m_[k_][:, 0:128])
    res = run_bass_kernel_spmd(nc, in_maps, core_ids=list(range(8)))
    out = np.zeros((2, S, D), np.float32)
    for core in range(8):
        b, r = core // 4, core % 4
        o = np.asarray(res.results[core]["outT"], np.float32)
        out[b].reshape(NB, 128, D)[r::4] = o.T.reshape(16, 128, D)
    return out
```

```python
import numpy as np
import ml_dtypes
import concourse.bass as bass
import concourse.mybir as mybir
from concourse.bass_utils import run_bass_kernel_spmd

F32 = mybir.dt.float32
BF16 = mybir.dt.bfloat16
AF = mybir.ActivationFunctionType
ALU = mybir.AluOpType
AX = mybir.AxisListType

D = 1024
S = 8192
NB = 64
OWN = 2048
DFF = 2816
NFC = 22
ALPHA = 2.0 ** 0.25
MASKV = -240.0
NEG_BIG = -30000.0
ARENA_F32 = 51200


class Op:
    __slots__ = ("id", "eng", "emit", "deps", "is_dma", "sem", "val", "sig")

    def __init__(self, id, eng, emit, deps, is_dma):
        self.id = id
        self.eng = eng
        self.emit = emit
        self.deps = deps
        self.is_dma = is_dma
        self.sem = None
        self.val = 0
        self.sig = False


class Prog:
    ENGS = ("pe", "act", "dve", "pool", "sp")
    EPOCH = 6000
    NDMA = 20

    def __init__(self):
        self.ops = []
        self.last_w = {}
        self.readers = {}
        self.dma_readers = {}
        self.eng_last = {e: None for e in self.ENGS}
        self.pending = {e: set() for e in self.ENGS}
        self.dma_open = []
        self.dma_last_on_sem = [None] * self.NDMA
        self.dma_rr = 0
        self.dma_uses = [0] * self.NDMA

    def add(self, eng, emit, reads=(), writes=(), dma=False):
        deps = set()
        for k in reads:
            w = self.last_w.get(k)
            if w is not None:
                deps.add(w)
        for k in writes:
            w = self.last_w.get(k)
            if w is not None:
                deps.add(w)
            deps |= set(self.readers.get(k, {}).values())
            deps |= self.dma_readers.get(k, set())
        deps |= self.pending[eng]
        self.pending[eng] = set()
        op = Op(len(self.ops), eng, emit, deps, dma)
        if dma:
            s = self.dma_rr
            self.dma_rr = (self.dma_rr + 1) % self.NDMA
            prev = self.dma_last_on_sem[s]
            if prev is not None:
                op.deps.add(prev)
            self.dma_last_on_sem[s] = op.id
            self.dma_uses[s] += 1
            op.sem = ("dma", s)
            op.val = 16 * self.dma_uses[s]
            op.sig = True
            self.dma_open.append(op.id)
        op.deps.discard(op.id)
        self.ops.append(op)
        for k in reads:
            if dma:
                self.dma_readers.setdefault(k, set()).add(op.id)
            else:
                self.readers.setdefault(k, {})[eng] = op.id
        for k in writes:
            self.last_w[k] = op.id
            self.readers[k] = {}
            self.dma_readers[k] = set()
        if not dma:
            self.eng_last[eng] = op.id
        return op.id

    def barrier(self):
        deps = set(self.dma_open)
        self.dma_open = []
        for e in self.ENGS:
            if self.eng_last[e] is not None:
                deps.add(self.eng_last[e])
        for e in self.ENGS:
            self.pending[e] |= deps

    def finalize(self):
        ops = self.ops
        for op in ops:
            for d in op.deps:
                p = ops[d]
                if p.is_dma:
                    continue
                if p.eng == "pe" and op.eng == "pe" and not op.is_dma:
                    continue
                p.sig = True
        cnt = {e: 0 for e in self.ENGS}
        self.n_epochs = {e: 1 for e in self.ENGS}
        for op in ops:
            if op.is_dma or not op.sig:
                continue
            c = cnt[op.eng]
            ep = c // self.EPOCH
            op.sem = (op.eng, ep)
            op.val = c - ep * self.EPOCH + 1
            cnt[op.eng] = c + 1
            self.n_epochs[op.eng] = ep + 1

    def sem_names(self):
        names = [("dma", i) for i in range(self.NDMA)]
        for e in self.ENGS:
            for ep in range(self.n_epochs[e]):
                names.append((e, ep))
        return names

    def emit_engine(self, eng, engine, sems):
        waited = {}
        ops = self.ops
        for op in ops:
            if op.eng != eng:
                continue
            need = {}
            for d in op.deps:
                p = ops[d]
                if (not p.is_dma) and p.eng == "pe" and eng == "pe" and not op.is_dma:
                    continue
                if p.sem is None:
                    continue
                if need.get(p.sem, 0) < p.val:
                    need[p.sem] = p.val
            for sname, v in need.items():
                if waited.get(sname, 0) >= v:
                    continue
                engine.wait_ge(sems[sname], v)
                waited[sname] = v
            if op.emit is None:
                continue
            ins = op.emit(engine)
            if op.sig:
                ins.then_inc(sems[op.sem], 16 if op.is_dma else 1)


class Arena:
    def __init__(self, ap, nbytes):
        self.ap = ap
        self.top = 0
        self.cap = nbytes
        self.peak = 0

    def alloc(self, shape, dtype):
        n = 1
        for s in shape[1:]:
            n *= s
        esz = 2 if dtype == BF16 else 4
        nb = (n * esz + 31) // 32 * 32
        off = self.top
        self.top += nb
        self.peak = max(self.peak, self.top)
        assert self.top <= self.cap, ("arena overflow", self.top, self.cap)
        v = self.ap[:, off // 4:(off + nb) // 4]
        if dtype == BF16:
            v = v.bitcast(BF16)
        v = v[:, 0:n]
        if len(shape) == 3:
            v = v.rearrange("p (a b) -> p a b", a=shape[1])
        elif len(shape) == 4:
            v = v.rearrange("p (a b c) -> p a b c", a=shape[1], b=shape[2])
        return v

    def mark(self):
        return self.top

    def release(self, m):
        self.top = m


def build_program(stop=None):
    nc = bass.Bass("TRN2", target_bir_lowering=False)

    def din(name, shape, dt=F32):
        return nc.dram_tensor(name, list(shape), dt, kind="ExternalInput").ap()

    early = stop is not None and (stop in ("mod", "projA", "swa") or stop.startswith("swa"))
    xT = din("xT", [D, 128 if early else S])
    xq = din("xq", [D, OWN])
    xh = din("xh", [D, OWN])
    w_ada = din("w_ada", [D, 6 * D])
    w_in = din("w_in", [D, 2304])
    w_out = din("w_out", [D, 128 if early else D])
    w_gu = din("w_gu", [D, 128 if early else 2 * DFF])
    w_down = din("w_down", [DFF, 128 if early else D])
    cst_d = din("cst", [128, 5, 128], BF16)
    mk_d = din("mk", [128, 4, 128], BF16)
    vec_d = din("vec", [128, 128])
    bvb_d = din("bvb", [128, 768])
    swab_d = din("swab", [128, 2, 8, 256])
    outT = nc.dram_tensor("outT", [D, OWN], F32, kind="ExternalOutput").ap()

    arena_t = nc.alloc_sbuf_tensor("arena", [128, ARENA_F32], F32)
    A = Arena(arena_t[:, :], ARENA_F32 * 4)
    ps = nc.alloc_psum_tensor("ps", [128, 4096], F32)

    def bank(b, n=1):
        return ps[:, b * 512:(b + n) * 512]

    P = Prog()

    def finish():
        P.finalize()
        names = P.sem_names()
        from contextlib import ExitStack
        with ExitStack() as es:
            sems = {}
            for nm in names:
                sems[nm] = es.enter_context(nc.semaphore("s_%s_%d" % nm))
            block = es.enter_context(nc.Block())

            @block.tensor
            def _(e):
                P.emit_engine("pe", e, sems)

            @block.scalar
            def _(e):
                P.emit_engine("act", e, sems)

            @block.vector
            def _(e):
                P.emit_engine("dve", e, sems)

            @block.gpsimd
            def _(e):
                P.emit_engine("pool", e, sems)

            @block.sync
            def _(e):
                P.emit_engine("sp", e, sems)
        return nc, P, A


    outT_dbg = outT.rearrange("(kc p) t -> p kc t", p=128)

    def dbg_stop(dumps):
        P.barrier()
        col = 0
        keys = []
        for i, (v, n) in enumerate(dumps):
            P.add("sp", lambda e, v=v, n=n, col=col: e.dma_start(out=outT_dbg[:, 0, col:col + n], in_=v), [], [("dbg", i)], dma=True)
            keys.append(("dbg", i))
            col += n
        P.add("sp", None, reads=keys, writes=[])
        return finish()

    cst = A.alloc([128, 5, 128], BF16)
    ident = cst[:, 0, :]
    Umat = cst[:, 1, :]
    negones = cst[:, 2, :]
    onesM = cst[:, 3, :]
    ones512 = cst[:, 4, :]
    Mk = A.alloc([128, 4, 128], BF16)
    vec = A.alloc([128, 128], F32)
    b_in_c = vec[:, 0:18]
    gn_c = vec[:, 18:26]
    ln1g = vec[:, 26:34]
    ln1b = vec[:, 34:42]
    ln2g = vec[:, 42:50]
    ln2b = vec[:, 50:58]
    sinks_c = vec[:, 58:66]
    b_ada_c = vec[:, 66:114]
    c_c = vec[:, 114:122]
    modT = A.alloc([128, 48], F32)
    onep = A.alloc([128, 48], F32)
    silu_c = A.alloc([128, 8], F32)
    tmp8 = A.alloc([128, 8], F32)
    A_small_bf = A.alloc([128, 8], BF16)
    bvb = A.alloc([128, 768], F32)
    bq8 = A.alloc([128, 18], F32)
    wstage_flat = [A.alloc([128, 2048], F32) for _ in range(2)]
    wstage = [w.rearrange("p (a b) -> p a b", a=8) for w in wstage_flat]
    off_ys = A.mark()
    ysb = A.alloc([128, 4, OWN], BF16)
    ysw = A.alloc([128, 4, OWN], BF16)
    mQT = A.mark()
    QT = A.alloc([128, 4, OWN], BF16)

    sh_a = modT[:, 0:8]
    onep_a = onep[:, 8:16]
    onep_ga = onep[:, 16:24]
    sh_f = modT[:, 24:32]
    onep_f = onep[:, 32:40]
    onep_gf = onep[:, 40:48]

    def MM(out, lhsT, rhs, start, stop):
        return lambda e: e.matmul(out, lhsT=lhsT, rhs=rhs, start=start, stop=stop, skip_group_check=True)

    def TR(out, in_):
        return lambda e: e.transpose(out, in_, ident)

    def ACTF(out, in_, func, bias=None, scale=None):
        kw = {}
        if bias is not None:
            kw["bias"] = bias
        if scale is not None:
            kw["scale"] = scale
        return lambda e: e.activation(out=out, in_=in_, func=func, **kw)

    def TT(out, in0, in1, op):
        return lambda e: e.tensor_tensor(out=out, in0=in0, in1=in1, op=op)

    def TS(out, in0, s1, s2, op0, op1=None):
        if op1 is None:
            return lambda e: e.tensor_scalar(out=out, in0=in0, scalar1=s1, scalar2=None, op0=op0)
        return lambda e: e.tensor_scalar(out=out, in0=in0, scalar1=s1, scalar2=s2, op0=op0, op1=op1)

    def STT(out, in0, scalar, in1, op0, op1):
        return lambda e: e.scalar_tensor_tensor(out=out, in0=in0, scalar=scalar, in1=in1, op0=op0, op1=op1)

    def COPY(out, in_):
        return lambda e: e.tensor_copy(out=out, in_=in_)

    def RED(out, in_, op):
        return lambda e: e.tensor_reduce(out=out, in_=in_, axis=AX.X, op=op)

    def RECIP(out, in_):
        return lambda e: e.reciprocal(out=out, in_=in_)

    def MEMSET(out, v):
        return lambda e: e.memset(out, v)

    def DMA(out, in_):
        return lambda e: e.dma_start(out=out, in_=in_)

    def dma(out, in_, reads, writes):
        return P.add("sp", DMA(out, in_), reads=reads, writes=writes, dma=True)

    dma(cst, cst_d[:, :, :], [], ["cst"])
    dma(Mk, mk_d[:, :, :], [], ["mk"])
    dma(vec, vec_d[:, :], [], ["vec"])
    dma(bvb, bvb_d[:, :], [], ["bvb"])

    cv_rr = [0]

    def load_weight(dst, src_view, ncols, key, dup=False):
        for c0 in range(0, ncols, 256):
            w = min(256, ncols - c0)
            sl = cv_rr[0] % 2
            st = wstage[sl]
            dma(st[:, :, 0:w], src_view[:, :, c0:c0 + w], [], [("wst", sl)])
            eng = "dve" if (cv_rr[0] % 2 == 0) else "pool"
            cv_rr[0] += 1
            if not dup:
                P.add(eng, COPY(dst[:, :, c0:c0 + w], st[:, :, 0:w]), [("wst", sl)], [key])
            else:
                for g in range(w // 64):
                    for d2 in range(2):
                        P.add(eng, COPY(dst[:, :, g * 128 + d2 * 64: g * 128 + d2 * 64 + 64], st[:, :, g * 64:(g + 1) * 64]),
                              [("wst", sl)], [key])

    mod_rr = [0]

    def modulate(xs_t, xb_t, skey, bkey, onep_v, sh_v):
        for kc in range(8):
            use_act = (mod_rr[0] % 3 == 2)
            mod_rr[0] += 1
            if use_act:
                P.add("act", ACTF(xb_t[:, kc, :], xs_t[:, kc, :], AF.Identity, bias=sh_v[:, kc:kc + 1], scale=onep_v[:, kc:kc + 1]),
                      [skey, "modA"], [(bkey, kc)])
            else:
                P.add("dve", TS(xb_t[:, kc, :], xs_t[:, kc, :], onep_v[:, kc:kc + 1], sh_v[:, kc:kc + 1], ALU.mult, ALU.add),
                      [skey, "modA"], [(bkey, kc)])

    bank_rr = [0]

    nbanks = [7]

    def next_bank():
        b = bank_rr[0] % nbanks[0]
        bank_rr[0] += 1
        return b

    w_in_v = w_in.rearrange("(kc p) n -> p kc n", p=128)
    xq_v = xq.rearrange("(kc p) t -> p kc t", p=128)
    xh_v = xh.rearrange("(kc p) t -> p kc t", p=128)
    xT_v = xT.rearrange("(kc p) t -> p kc t", p=128)
    chunk_id = [0]

    def stage_chunk(xs_l, xb_l, src):
        i = chunk_id[0]
        chunk_id[0] += 1
        sl = i % 2
        dma(xs_l[sl], src, [], [("xs", sl)])
        modulate(xs_l[sl], xb_l[sl], ("xs", sl), ("xb", sl), onep_a, sh_a)
        return xb_l[sl], [(("xb", sl), kc) for kc in range(8)]

    def proj_fm(xb_t, xkeys, W, wkey, c0):
        b = next_bank()
        for kc in range(8):
            P.add("pe", MM(bank(b), W[:, kc, c0:c0 + 128], xb_t[:, kc, :], kc == 0, kc == 7), [wkey, xkeys[kc]], [("bank", b)])
        return b

    def proj_tm(xb_t, xkeys, W, wkey, i, ncols):
        b = next_bank()
        for kc in range(8):
            P.add("pe", MM(bank(b)[:, 0:ncols], xb_t[:, kc, i * 128:(i + 1) * 128], W[:, kc, 0:ncols], kc == 0, kc == 7),
                  [wkey, xkeys[kc]], [("bank", b)])
        return b

    mA = A.mark()
    Wq = A.alloc([128, 8, 512], BF16)
    Wqs = A.alloc([128, 8, 512], BF16)
    Wks = A.alloc([128, 8, 256], BF16)
    Wvs = A.alloc([128, 8, 256], BF16)
    qsT = A.alloc([128, 4, OWN], BF16)
    kcat = A.alloc([128, 2, 16, 256], BF16)
    vsw = A.alloc([128, 16, 2, 256], BF16)
    m0 = A.mark()
    wa_stage = [A.alloc([128, 8, 512], F32) for _ in range(2)]
    wab0 = [A.alloc([128, 8, 512], BF16) for _ in range(2)]
    silu_bf = A_small_bf
    P.add("act", ACTF(tmp8, c_c, AF.Exp, scale=-1.0), ["vec"], ["tmp8"])
    P.add("dve", TS(tmp8, tmp8, 1.0, None, ALU.add), ["tmp8"], ["tmp8"])
    P.add("dve", RECIP(tmp8, tmp8), ["tmp8"], ["tmp8"])
    P.add("dve", TT(silu_c, c_c, tmp8, ALU.mult), ["tmp8", "vec"], ["silu"])
    P.add("dve", COPY(silu_bf, silu_c), ["silu"], ["silubf"])
    w_ada_v = w_ada.rearrange("(kc p) n -> p kc n", p=128)
    modp = bank(7)[:, 0:48]

    def mod_cols(stb, skey, col, c_in_piece):
        for kc in range(8):
            P.add("pe", MM(modp[:, col:col + 1], stb[:, kc, c_in_piece * 128:(c_in_piece + 1) * 128], silu_bf[:, kc:kc + 1], kc == 0, kc == 7),
                  [skey, "silubf"], [("modp", col)])

    for pc in range(4):
        st = wa_stage[pc % 2]
        dma(st, w_ada_v[:, :, pc * 512:(pc + 1) * 512], [], [("wa", pc % 2)])
        if pc == 1:
            load_weight(Wq, w_in_v[:, :, 0:512], 512, "Wq")
            load_weight(Wqs, w_in_v[:, :, 1536:2048], 512, "Wqs")
            load_weight(Wks, w_in_v[:, :, 2048:2176], 128, "Wks", dup=True)
            load_weight(Wvs, w_in_v[:, :, 2176:2304], 128, "Wvs", dup=True)
        P.add("dve", COPY(wab0[pc % 2], st), [("wa", pc % 2)], [("wab0", pc % 2)])
        for cc in range(4):
            mod_cols(wab0[pc % 2], ("wab0", pc % 2), pc * 4 + cc, cc)
    P.add("dve", TT(modT[:, 0:16], modp[:, 0:16], b_ada_c[:, 0:16], ALU.add), [("modp", c) for c in range(16)] + ["vec"], ["modA"])
    P.add("dve", TS(onep[:, 0:16], modT[:, 0:16], 1.0, None, ALU.add), ["modA"], ["modA"])
    P.add("dve", TS(bq8, b_in_c, 0.125, None, ALU.mult), ["vec"], ["bq8"])
    P.barrier()
    A.release(m0)
    if stop == "mod":
        return dbg_stop([(modT[:, 0:16], 16), (onep[:, 0:16], 16), (silu_c, 8)])

    mA2 = A.mark()
    xsA = [A.alloc([128, 8, 512], F32) for _ in range(2)]
    xbA = [A.alloc([128, 8, 512], BF16) for _ in range(2)]
    bk_sw = bvb[:, 512:768]

    for t in range(4):
        xb_t, xk = stage_chunk(xsA, xbA, xq_v[:, :, t * 512:(t + 1) * 512])
        for hp in range(4):
            b = proj_fm(xb_t, xk, Wq, "Wq", hp * 128)
            P.add("act", ACTF(QT[:, hp, t * 512:(t + 1) * 512], bank(b), AF.Identity, bias=bq8[:, hp:hp + 1], scale=0.125),
                  [("bank", b), "bq8"], [("QT", hp, t)])
        for hp in range(4):
            b = proj_fm(xb_t, xk, Wqs, "Wqs", hp * 128)
            P.add("act", ACTF(qsT[:, hp, t * 512:(t + 1) * 512], bank(b), AF.Identity, bias=b_in_c[:, 12 + hp:13 + hp], scale=1.0),
                  [("bank", b), "vec"], [("qsT", hp, t)])
        for g in range(2):
            b = proj_fm(xb_t, xk, Wks, "Wks", g * 128)
            P.add("dve", TS(kcat[:, g, 4 * t:4 * t + 4, 128:256], bank(b).rearrange("p (a b) -> p a b", a=4),
                            vec[:, 122 + g:123 + g], None, ALU.add), [("bank", b), "vec"], [("kcat", g, t, 1)])
        for i in range(4):
            b = proj_tm(xb_t, xk, Wvs, "Wvs", i, 256)
            P.add("dve", TT(vsw[:, 4 * t + i, 1, :], bank(b)[:, 0:256], bk_sw, ALU.add), [("bank", b), "bvb"], [("vsw", 4 * t + i, 1)])
    for t in range(4):
        xb_t, xk = stage_chunk(xsA, xbA, xh_v[:, :, t * 512:(t + 1) * 512])
        for g in range(2):
            b = proj_fm(xb_t, xk, Wks, "Wks", g * 128)
            P.add("dve", TS(kcat[:, g, 4 * t:4 * t + 4, 0:128], bank(b).rearrange("p (a b) -> p a b", a=4),
                            vec[:, 122 + g:123 + g], None, ALU.add), [("bank", b), "vec"], [("kcat", g, t, 0)])
        for i in range(4):
            b = proj_tm(xb_t, xk, Wvs, "Wvs", i, 256)
            P.add("dve", TT(vsw[:, 4 * t + i, 0, :], bank(b)[:, 0:256], bk_sw, ALU.add), [("bank", b), "bvb"], [("vsw", 4 * t + i, 0)])
    P.barrier()
    A.release(mA2)
    if stop == "projA":
        return dbg_stop([(modT, 48)])

    swab = A.alloc([128, 2, 8, 256], F32)
    dma(swab, swab_d[:, :, :, :], [], ["swab"])
    if stop == "swa0a":
        return dbg_stop([(swab[:, 0, 0, :], 256)])
    Ssb = [A.alloc([128, 4, 256], F32) for _ in range(2)]
    pn = [A.alloc([128, 4, 256], BF16) for _ in range(2)]
    pT = [A.alloc([128, 8, 128], BF16) for _ in range(2)]
    st4 = [A.alloc([128, 16], F32) for _ in range(2)]
    wab = [A.alloc([128, 8, 256], BF16) for _ in range(2)]
    def ACTF_ACC(out, in_, func, bias, acc):
        return lambda e: e.activation(out=out, in_=in_, func=func, bias=bias, scale=1.0, accum_out=acc)

    def swa_ctx(it):
        m, g = it // 2, it % 2
        sl = it % 2
        bS = (0, 1) if sl == 0 else (4, 5)
        bT = 2
        bO = 3 if sl == 0 else 6
        Sps = ps[:, bS[0] * 512:bS[0] * 512 + 1024].rearrange("p (a b) -> p a b", a=4)
        return m, g, sl, bS, bT, bO, Sps

    def swa_S(it):
        m, g, sl, bS, bT, bO, Sps = swa_ctx(it)
        for hh in range(4):
            h = 4 * g + hh
            pair, half = h // 2, h % 2
            a = half * 2 + hh // 2
            P.add("pe", MM(Sps[:, a, :], qsT[half * 64:(half + 1) * 64, pair, m * 128:(m + 1) * 128],
                           kcat[half * 64:(half + 1) * 64, g, m, :], True, True),
                  [("qsT", pair, m // 4), ("kcat", g, m // 4, 0), ("kcat", g, m // 4, 1)], [("bank", bS[half])])

    def swa_front(it):
        m, g, sl, bS, bT, bO, Sps = swa_ctx(it)
        Sv, sv = Ssb[sl], st4[sl]
        mx = sv[:, 0:4]
        rs = sv[:, 4:8]
        es = sv[:, 8:12]
        dd = sv[:, 12:16]
        skey = ("Ssb", sl)
        stk = ("st", sl)
        bsel = 0 if m == 0 else 1
        P.add("dve", STT(Sv, Sps, 0.125, swab[:, bsel, 4 * g:4 * g + 4, :], ALU.mult, ALU.add),
              [("bank", bS[0]), ("bank", bS[1]), "swab"], [skey])
        P.add("dve", RED(mx, Sv, ALU.max), [skey], [stk])
        P.add("dve", TT(mx, mx, sinks_c[:, 4 * g:4 * g + 4], ALU.max), [stk, "vec"], [stk])
        P.add("dve", TS(mx, mx, -1.0, None, ALU.mult), [stk], [stk])
        P.add("dve", TT(dd, sinks_c[:, 4 * g:4 * g + 4], mx, ALU.add), [stk, "vec"], [stk])
        for a in range(4):
            P.add("act", ACTF_ACC(Sv[:, a, :], Sv[:, a, :], AF.Exp, mx[:, a:a + 1], rs[:, a:a + 1]), [skey, stk], [skey, ("rs", sl)])
        P.add("act", ACTF(es, dd, AF.Exp), [stk, ("rs", sl)], [("es", sl)])

    def swa_back(it):
        m, g, sl, bS, bT, bO, Sps = swa_ctx(it)
        Sv, pnv, sv = Ssb[sl], pn[sl], st4[sl]
        rs = sv[:, 4:8]
        es = sv[:, 8:12]
        skey = ("Ssb", sl)
        P.add("dve", TT(rs, rs, es, ALU.add), [("es", sl), ("rs", sl)], [("rs", sl)])
        P.add("dve", RECIP(rs, rs), [("rs", sl)], [("rs", sl)])
        P.add("dve", TT(pnv, Sv, rs.unsqueeze(2).to_broadcast([128, 4, 256]), ALU.mult), [skey, ("rs", sl)], [("pn", sl)])

    def swa_tr(it):
        m, g, sl, bS, bT, bO, Sps = swa_ctx(it)
        pnv, pTv = pn[sl], pT[sl]
        Tps = bank(bT).bitcast(BF16)
        for a in range(4):
            for blk in range(2):
                idx = a * 2 + blk
                P.add("pe", TR(Tps[:, idx * 128:(idx + 1) * 128], pnv[:, a, blk * 128:(blk + 1) * 128]),
                      [("pn", sl), "cst"], [("bank", bT)])
        P.add("act", ACTF(pTv, Tps.rearrange("p (a b) -> p a b", a=8), AF.Copy), [("bank", bT)], [("pT", sl)])

    def swa_pv(it):
        m, g, sl, bS, bT, bO, Sps = swa_ctx(it)
        pTv = pT[sl]
        Ops = bank(bO)
        for a in range(4):
            for blk in range(2):
                P.add("pe", MM(Ops[:, a * 128:(a + 1) * 128], vsw[:, m, blk, g * 128:(g + 1) * 128], pTv[:, a * 2 + blk, :],
                               blk == 0, blk == 1), [("pT", sl), ("vsw", m, blk)], [("bank", bO)])
        Ov = Ops.rearrange("p (h t) -> p h t", h=4)
        P.add("act", ACTF(ysw[0:64, 2 * g:2 * g + 2, m * 128:(m + 1) * 128], Ov[0:64, 0:2, :], AF.Copy),
              [("bank", bO)], [("ysw", m, g, 0)])
        P.add("act", ACTF(ysw[64:128, 2 * g:2 * g + 2, m * 128:(m + 1) * 128], Ov[64:128, 2:4, :], AF.Copy),
              [("bank", bO)], [("ysw", m, g, 1)])

    swa_S(0)
    swa_front(0)
    swa_S(1)
    for it in range(32):
        if it + 1 < 32:
            swa_front(it + 1)
        swa_back(it)
        if it + 2 < 32:
            swa_S(it + 2)
        col = 16 + it
        wsl = (it // 2) % 2
        if it % 2 == 0:
            dma(wstage[wsl], w_ada_v[:, :, col * 128:col * 128 + 256], [], [("wst", wsl)])
            P.add("dve", COPY(wab[wsl], wstage[wsl]), [("wst", wsl)], [("wab", wsl)])
        mod_cols(wab[wsl], ("wab", wsl), col, it % 2)
        swa_tr(it)
        swa_pv(it)
    P.add("dve", TT(modT[:, 16:48], modp[:, 16:48], b_ada_c[:, 16:48], ALU.add), [("modp", c) for c in range(16, 48)] + ["vec"], ["modB"])
    P.add("dve", TS(onep[:, 16:48], modT[:, 16:48], 1.0, None, ALU.add), ["modB"], ["modB"])
    P.barrier()
    A.release(mA)
    nbanks[0] = 8
    if stop == "swa":
        return dbg_stop([(modT, 48)])

    for hf in range(2):
        mB = A.mark()
        KT = A.alloc([128, 2, S], BF16)
        Vt = A.alloc([128, NB, 256], BF16)
        mB2 = A.mark()
        Wk = A.alloc([128, 8, 256], BF16)
        Wv = A.alloc([128, 8, 256], BF16)
        xsB = [A.alloc([128, 8, 512], F32) for _ in range(2)]
        xbB = [A.alloc([128, 8, 512], BF16) for _ in range(2)]
        load_weight(Wk, w_in_v[:, :, 512 + hf * 256:512 + (hf + 1) * 256], 256, "Wk")
        load_weight(Wv, w_in_v[:, :, 1024 + hf * 256:1024 + (hf + 1) * 256], 256, "Wv")
        bv_sb = bvb[:, hf * 256:(hf + 1) * 256]
        for tc in range(16):
            xb_t, xk = stage_chunk(xsB, xbB, xT_v[:, :, tc * 512:(tc + 1) * 512])
            for lp in range(2):
                b = proj_fm(xb_t, xk, Wk, "Wk", lp * 128)
                P.add("act", ACTF(KT[:, lp, tc * 512:(tc + 1) * 512], bank(b), AF.Identity,
                                  bias=b_in_c[:, 4 + 2 * hf + lp:5 + 2 * hf + lp], scale=1.0), [("bank", b), "vec"], [("KT", lp, tc)])
            for i in range(4):
                b = proj_tm(xb_t, xk, Wv, "Wv", i, 256)
                P.add("dve", TT(Vt[:, 4 * tc + i, :], bank(b)[:, 0:256], bv_sb, ALU.add), [("bank", b), "bvb"], [("Vt", tc)])
        P.barrier()
        A.release(mB2)
        if stop == "projB":
            return dbg_stop([(modT, 48)])

        Esb = [A.alloc([128, 2, 512], F32) for _ in range(2)]
        Lsb = [A.alloc([128, 2, 512], BF16) for _ in range(2)]
        Tsb = [A.alloc([128, 2, 512], F32) for _ in range(3)]
        Psb = [A.alloc([128, 2, 512], BF16) for _ in range(2)]
        Rneg = [A.alloc([128, 2, 512], F32) for _ in range(2)]
        Ybk = [ps[:, 0:1024].rearrange("p (a b) -> p a b", a=2), ps[:, 1024:2048].rearrange("p (a b) -> p a b", a=2)]
        RB = ps[:, 2048:3072].rearrange("p (a b) -> p a b", a=2)
        Oab = [bank(6), bank(7)]

        def sb_stream(lp, hp, M, KT=KT, Vt=Vt, Esb=Esb, Lsb=Lsb, Tsb=Tsb, Psb=Psb, Rneg=Rneg):
            n = 16 * M + 16
            J = 16 * M + 15

            def info(k):
                j = J - k
                p = j - 16 * M
                q0 = (p // 4) if p >= 0 else 0
                return j, p, q0 * 128

            def emit_Z(k):
                j, p, c0 = info(k)
                s = k % 2
                for hd in range(2):
                    P.add("pe", MM(Ybk[s][:, hd, c0:512], KT[hd * 64:(hd + 1) * 64, lp, j * 128:(j + 1) * 128],
                                   QT[hd * 64:(hd + 1) * 64, hp, M * 512 + c0:M * 512 + 512], True, False),
                          [("KT", lp, j // 4), ("QT", hp, M)], [("Y", s)])
                if p >= 0:
                    q, ee = p // 4, p % 4
                    for hd in range(2):
                        P.add("pe", MM(Ybk[s][:, hd, q * 128:(q + 1) * 128], ident, Mk[:, ee, :], False, False),
                              ["cst", "mk"], [("Y", s)])

            def emit_E(k):
                j, p, c0 = info(k)
                s = k % 2
                P.add("act", ACTF(Esb[s][:, :, c0:512], Ybk[s][:, :, c0:512], AF.Exp), [("Y", s)], [("E", s)])

            def emit_L(k):
                j, p, c0 = info(k)
                s = k % 2
                P.add("act", ACTF(Lsb[s][:, :, c0:512], Esb[s][:, :, c0:512], AF.Ln, bias=1.0, scale=1.0), [("E", s)], [("L", s)])

            def emit_UO(k):
                j, p, c0 = info(k)
                s = k % 2
                for hd in range(2):
                    P.add("pe", MM(Ybk[s][:, hd, c0:512], Umat, Lsb[s][:, hd, c0:512], False, True), ["cst", ("L", s)], [("Y", s)])
                for hd in range(2):
                    P.add("pe", MM(RB[:, hd, c0:512], negones, Lsb[s][:, hd, c0:512], True, True), ["cst", ("L", s)], ["RB"])

            def emit_T(k):
                j, p, c0 = info(k)
                s = k % 2
                P.add("dve", TT(Tsb[k % 3][:, :, c0:512], Ybk[s][:, :, c0:512], Rneg[k % 2][:, :, c0:512], ALU.add),
                      [("Y", s), ("R", k % 2)], [("T", k % 3)])
                P.add("dve", TT(Rneg[(k + 1) % 2][:, :, c0:512], RB[:, :, c0:512], Rneg[k % 2][:, :, c0:512], ALU.add),
                      ["RB", ("R", k % 2)], [("R", (k + 1) % 2)])

            def emit_P(k):
                j, p, c0 = info(k)
                s = k % 2
                P.add("act", ACTF(Psb[s][:, :, c0:512], Tsb[k % 3][:, :, c0:512], AF.Exp), [("T", k % 3)], [("P", s)])

            def emit_PV(k):
                j, p, c0 = info(k)
                s = k % 2
                for hd in range(2):
                    P.add("pe", MM(Oab[hd][:, c0:512], Vt[:, j, lp * 128:(lp + 1) * 128], Psb[s][:, hd, c0:512], k == 0, k == n - 1),
                          [("Vt", j // 4), ("P", s)], [("O", hd)])

            P.add("pool", MEMSET(Rneg[0], 0.0), [], [("R", 0)])
            P.add("pool", MEMSET(Rneg[1], 0.0), [], [("R", 1)])
            emit_Z(0)
            for k in range(n + 2):
                if k < n:
                    emit_E(k)
                if k + 1 < n:
                    emit_Z(k + 1)
                if k - 2 >= 0:
                    emit_P(k - 2)
                    emit_PV(k - 2)
                if k < n:
                    emit_L(k)
                    emit_UO(k)
                    emit_T(k)
            P.add("dve", COPY(ysb[0:64, hp, M * 512:(M + 1) * 512], Oab[0][0:64, :]), [("O", 0)], [("ysb", hp, M, 0)])
            P.add("dve", COPY(ysb[64:128, hp, M * 512:(M + 1) * 512], Oab[1][64:128, :]), [("O", 1)], [("ysb", hp, M, 1)])

        for lp in range(2):
            for M in range(4):
                sb_stream(lp, 2 * hf + lp, M)
        P.barrier()
        A.release(mB)
        if stop == "sb0":
            return dbg_stop([(modT, 48)])

    A.release(mQT)
    x1T = A.alloc([128, 8, OWN], F32)
    mX = A.mark()
    Wo = A.alloc([128, 8, 1024], BF16)
    load_weight(Wo, w_out.rearrange("(kc p) n -> p kc n", p=128), 1024, "Wo")
    xres = A.alloc([128, 8, 512], F32)
    mixed = A.alloc([128, 8, 512], BF16)
    sqbC = A.alloc([128, 8, 512], BF16)
    ubfC = A.alloc([128, 8, 512], BF16)
    sttC = [A.alloc([128, 512], F32) for _ in range(4)]

    def layer_norm_tile(tok0, gcols, bcols, ubf, sqb, stt):
        uv = x1T[:, :, tok0:tok0 + 512]
        ukeys = [("x1", tok0 // 512, c) for c in range(8)]
        P.add("dve", COPY(ubf, uv), ukeys, ["ubf"])
        P.add("act", ACTF(sqb, uv, AF.Square), ukeys, ["sqb"])
        bm, bq = next_bank(), next_bank()
        for c in range(8):
            P.add("pe", MM(bank(bm), onesM, ubf[:, c, :], c == 0, c == 7), ["cst", "ubf"], [("bank", bm)])
        for c in range(8):
            P.add("pe", MM(bank(bq), onesM, sqb[:, c, :], c == 0, c == 7), ["cst", "sqb"], [("bank", bq)])
        m2, var, rstd, nmr = stt
        P.add("act", ACTF(m2, bank(bm), AF.Square), [("bank", bm)], ["m2"])
        P.add("dve", TT(var, bank(bq), m2, ALU.subtract), [("bank", bq), "m2"], ["var"])
        P.add("act", ACTF(var, var, AF.Ln, bias=1e-5, scale=1.0), ["var"], ["var"])
        P.add("act", ACTF(rstd, var, AF.Exp, scale=-0.5), ["var"], ["rstd"])
        P.add("dve", STT(nmr, bank(bm), -1.0, rstd, ALU.mult, ALU.mult), [("bank", bm), "rstd"], ["nmr"])
        for c in range(8):
            eng = "dve"
            ukey = ukeys[c]
            P.add(eng, TT(uv[:, c, :], uv[:, c, :], rstd, ALU.mult), [ukey, "rstd"], [ukey])
            P.add(eng, TT(uv[:, c, :], uv[:, c, :], nmr, ALU.add), [ukey, "nmr"], [ukey])
            P.add("act", ACTF(uv[:, c, :], uv[:, c, :], AF.Identity, bias=bcols[:, c:c + 1], scale=gcols[:, c:c + 1]), [ukey, "vec"], [ukey])

    for tt in range(4):
        tok0 = tt * 512
        dma(xres, xq_v[:, :, tok0:tok0 + 512], [], ["xres"])
        for G, yv in ((0, ysb), (1, ysw)):
            yt = yv[:, :, tok0:tok0 + 512]
            P.add("dve", TT(sqbC[:, 4 * G:4 * G + 4, :], yt, yt, ALU.mult), [], [("sq", G)])
            bms = next_bank()
            for c in range(4):
                P.add("pe", MM(bank(bms), ones512, sqbC[:, 4 * G + c, :], c == 0, c == 3), ["cst", ("sq", G)], [("bank", bms)])
            rt = sttC[G]
            P.add("act", ACTF(rt, bank(bms), AF.Ln, bias=1e-6, scale=1.0), [("bank", bms)], [("rt", G)])
            P.add("act", ACTF(rt, rt, AF.Exp, scale=-0.5), [("rt", G)], [("rt", G)])
            for c in range(4):
                P.add("dve", STT(mixed[:, 4 * G + c, :], yt[:, c, :], gn_c[:, 4 * G + c:4 * G + c + 1], rt, ALU.mult, ALU.mult),
                      [("rt", G), "vec"], [("mixed", 4 * G + c)])
        P.add("act", ACTF(xres, xres, AF.Identity, scale=ALPHA), ["xres"], ["xres"])
        for co in range(8):
            b = next_bank()
            for c in range(8):
                P.add("pe", MM(bank(b), Wo[:, c, co * 128:(co + 1) * 128], mixed[:, c, :], c == 0, c == 7), ["Wo", ("mixed", c)], [("bank", b)])
            P.add("dve", STT(x1T[:, co, tok0:tok0 + 512], bank(b), onep_ga[:, co:co + 1], xres[:, co, :], ALU.mult, ALU.add),
                  [("bank", b), "xres", "modB"], [("x1", tt, co)])
        layer_norm_tile(tok0, ln1g, ln1b, ubfC, sqbC, sttC)
    P.barrier()
    A.release(mX)
    if stop == "phC":
        return dbg_stop([(x1T[:, 0, :], 2048)])

    sqbD = A.alloc([128, 8, 512], BF16)
    ubfD = A.alloc([128, 8, 512], BF16)
    sttD = [A.alloc([128, 512], F32) for _ in range(4)]
    actT = A.alloc([128, NFC, 1024], BF16)
    Wd = [A.alloc([128, NFC, 128], BF16) for _ in range(2)]
    A2 = Arena(A.ap[:, off_ys // 4:(off_ys + 32768) // 4], 32768)
    h2s = A2.alloc([128, 2, 8, 512], BF16)
    Wgu = [A2.alloc([128, 8, 256], BF16) for _ in range(2)]
    sg = [A2.alloc([128, 512], F32) for _ in range(2)]
    w_gu_v = w_gu.rearrange("(kc p) n -> p kc n", p=128)
    w_down_v = w_down.rearrange("(fc p) n -> p fc n", p=128)
    outT_v = outT.rearrange("(kc p) t -> p kc t", p=128)
    wg_i = 0
    wd_i = 0
    sg_i = 0
    for TT_ in range(2):
        for sub in range(2):
            tok0 = TT_ * 1024 + sub * 512
            for c in range(8):
                eng = "dve" if c % 2 == 0 else "pool"
                P.add(eng, TS(h2s[:, sub, c, :], x1T[:, c, tok0:tok0 + 512], onep_f[:, c:c + 1], sh_f[:, c:c + 1], ALU.mult, ALU.add),
                      [("x1", tok0 // 512, c), "modB"], [("h2", sub, c)])
        for fc in range(NFC):
            sl = wg_i % 2
            wg_i += 1
            wsl = cv_rr[0] % 2
            cv_rr[0] += 1
            st = wstage[wsl]
            skey = ("wst", wsl)
            dma(st[:, :, 0:128], w_gu_v[:, :, fc * 128:(fc + 1) * 128], [], [skey])
            dma(st[:, :, 128:256], w_gu_v[:, :, DFF + fc * 128:DFF + (fc + 1) * 128], [], [skey])
            P.add("pool", COPY(Wgu[sl], st), [skey], [("Wgu", sl)])
            for sub in range(2):
                bg, bu = next_bank(), next_bank()
                for kc in range(8):
                    P.add("pe", MM(bank(bg), Wgu[sl][:, kc, 0:128], h2s[:, sub, kc, :], kc == 0, kc == 7),
                          [("Wgu", sl), ("h2", sub, kc)], [("bank", bg)])
                for kc in range(8):
                    P.add("pe", MM(bank(bu), Wgu[sl][:, kc, 128:256], h2s[:, sub, kc, :], kc == 0, kc == 7),
                          [("Wgu", sl), ("h2", sub, kc)], [("bank", bu)])
                ss = sg_i % 2
                sg_i += 1
                P.add("act", ACTF(sg[ss], bank(bg), AF.Silu), [("bank", bg)], [("sg", ss)])
                P.add("dve", TT(actT[:, fc, sub * 512:(sub + 1) * 512], sg[ss], bank(bu), ALU.mult), [("sg", ss), ("bank", bu)], [("act", fc, sub)])
        for sub in range(2):
            tok0 = TT_ * 1024 + sub * 512
            P.add("act", ACTF(x1T[:, :, tok0:tok0 + 512], x1T[:, :, tok0:tok0 + 512], AF.Identity, scale=ALPHA),
                  [("x1", tok0 // 512, c) for c in range(8)] + [("h2", sub, c) for c in range(8)],
                  [("x1", tok0 // 512, c) for c in range(8)])
        for co in range(8):
            sl = wd_i % 2
            wd_i += 1
            for h2_ in range(2):
                wsl = cv_rr[0] % 2
                cv_rr[0] += 1
                stv = wstage_flat[wsl][:, 0:11 * 128].rearrange("p (a b) -> p a b", a=11)
                dma(stv, w_down_v[:, h2_ * 11:(h2_ + 1) * 11, co * 128:(co + 1) * 128], [], [("wst", wsl)])
                P.add("pool" if h2_ == 0 else "dve", COPY(Wd[sl][:, h2_ * 11:(h2_ + 1) * 11, :], stv), [("wst", wsl)], [("Wd", sl, h2_)])
            for sub in range(2):
                tok0 = TT_ * 1024 + sub * 512
                b = next_bank()
                for fc in range(NFC):
                    P.add("pe", MM(bank(b), Wd[sl][:, fc, :], actT[:, fc, sub * 512:(sub + 1) * 512], fc == 0, fc == NFC - 1),
                          [("Wd", sl, fc // 11), ("act", fc, sub)], [("bank", b)])
                P.add("dve", STT(x1T[:, co, tok0:tok0 + 512], bank(b), onep_gf[:, co:co + 1], x1T[:, co, tok0:tok0 + 512], ALU.mult, ALU.add),
                      [("bank", b), ("x1", tok0 // 512, co), "modB"], [("x1", tok0 // 512, co)])
        for sub in range(2):
            tok0 = TT_ * 1024 + sub * 512
            layer_norm_tile(tok0, ln2g, ln2b, ubfD, sqbD, sttD)
            P.add("sp", DMA(outT_v[:, :, tok0:tok0 + 512], x1T[:, :, tok0:tok0 + 512]),
                  reads=[("x1", tok0 // 512, c) for c in range(8)], writes=[("out", tok0)], dma=True)
    P.add("sp", None, reads=[("out", t * 512) for t in range(4)], writes=[])
    print("arena peak bytes", A.peak, "n_ops", len(P.ops))
    return finish()


_CACHE = {}


def _consts():
    bf = ml_dtypes.bfloat16
    cst = np.zeros((128, 5, 128), np.float32)
    cst[:, 0, :] = np.eye(128)
    sp, s_ = np.meshgrid(np.arange(128), np.arange(128), indexing="ij")
    cst[:, 1, :] = -(sp >= s_).astype(np.float32)
    cst[:, 2, :] = -1.0
    cst[:, 3, :] = 1.0 / 1024.0
    cst[:, 4, :] = 1.0 / 512.0
    return cst.astype(bf)


def kernel(x, c, w_ada, b_ada, w_in, b_in, sinks, gn_sb, gn_swa, w_out,
           ln1_g, ln1_b, w_gu, w_down, ln2_g, ln2_b):
    bf = ml_dtypes.bfloat16
    x = np.asarray(x, np.float32)
    f = lambda a: np.ascontiguousarray(np.asarray(a, np.float32))
    if "nc" not in _CACHE:
        _CACHE["nc"] = build_program()
    nc, P, A = _CACHE["nc"]

    cst = _consts()
    col = lambda v: np.asarray(v, np.float32).reshape(-1, 128).T
    b_in0 = np.asarray(b_in[0], np.float32)
    shared = {
        "w_ada": f(w_ada[0]), "w_in": f(w_in[0]), "w_out": f(w_out[0]), "w_gu": f(w_gu[0]), "w_down": f(w_down[0]),
        "cst": cst,
    }
    bvb = np.zeros((128, 768), np.float32)
    bvb[:, 0:512] = b_in0[1024:1536][None, :]
    bvsw = b_in0[2176:2304].reshape(2, 64)
    bvb[:, 512:768] = np.concatenate([bvsw[0], bvsw[0], bvsw[1], bvsw[1]])[None, :]
    shared["bvb"] = bvb
    t_i = np.arange(128)[:, None]
    s_i = np.arange(256)[None, :]
    dist = (t_i + 128 - s_i).astype(np.float32)
    inband = (dist >= 0) & (dist < 128)
    slopes = np.exp2(-8.0 * np.arange(1, 9, dtype=np.float32) / 8.0)
    base = np.where(inband[None], -slopes[:, None, None] * dist[None], NEG_BIG).astype(np.float32)
    base = np.transpose(base, (1, 0, 2))
    HPERM = np.array([0, 2, 1, 3, 4, 6, 5, 7])
    base = np.ascontiguousarray(base[:, HPERM, :])
    nohalo = base.copy()
    nohalo[:, :, 0:128] = NEG_BIG

    in_maps = []
    for core in range(8):
        b, r = core // 4, core % 4
        xb_ = x[b]
        blocks = xb_.reshape(NB, 128, D)
        own = blocks[r::4]
        halo_idx = np.arange(16) * 4 + r - 1
        halo = np.zeros_like(own)
        valid = halo_idx >= 0
        halo[valid] = blocks[halo_idx[valid]]
        vecs = np.zeros((128, 128), np.float32)
        vecs[:, 0:18] = col(b_in0)
        vecs[:, 18:22] = col(gn_sb[0])
        vecs[:, 22:26] = col(gn_swa[0])
        vecs[:, 26:34] = col(ln1_g[0])
        vecs[:, 34:42] = col(ln1_b[0])
        vecs[:, 42:50] = col(ln2_g[0])
        vecs[:, 50:58] = col(ln2_b[0])
        vecs[:, 58:66] = np.asarray(sinks[0], np.float32)[HPERM][None, :]
        vecs[:, 66:114] = col(b_ada[0])
        vecs[:, 114:122] = col(c[b])
        bks = b_in0[2048:2176].reshape(2, 64)
        vecs[:, 122] = np.concatenate([bks[0], bks[0]])
        vecs[:, 123] = np.concatenate([bks[1], bks[1]])
        mk = np.zeros((128, 4, 128), np.float32)
        ss, tt = np.meshgrid(np.arange(128), np.arange(128), indexing="ij")
        for e in range(4):
            if e == r:
                mk[:, e, :] = np.where(ss >= tt, MASKV, 0.0)
            elif e > r:
                mk[:, e, :] = MASKV
        swab = np.stack([nohalo if r == 0 else base, base], axis=1)
        m = dict(shared)
        m.update({
            "xT": np.ascontiguousarray(xb_.T),
            "xq": np.ascontiguousarray(own.reshape(OWN, D).T),
            "xh": np.ascontiguousarray(halo.reshape(OWN, D).T),
            "mk": mk.astype(bf),
            "vec": vecs,
            "swab": np.ascontiguousarray(swab),
        })
        in_maps.append(m)

    if _CACHE.get("early"):
        for m_ in in_maps:
            for k_ in ("xT", "w_out", "w_gu", "w_down"):
                m_[k_] = np.ascontiguousarray(m_[k_][:, 0:128])
    res = run_bass_kernel_spmd(nc, in_maps, core_ids=list(range(8)))
    out = np.zeros((2, S, D), np.float32)
    for core in range(8):
        b, r = core // 4, core % 4
        o = np.asarray(res.results[core]["outT"], np.float32)
        out[b].reshape(NB, 128, D)[r::4] = o.T.reshape(16, 128, D)
    return out
```

```python
import numpy as np
import ml_dtypes
import concourse.bass as bass
import concourse.mybir as mybir
from concourse.bass_utils import run_bass_kernel_spmd

F32 = mybir.dt.float32
BF16 = mybir.dt.bfloat16
AF = mybir.ActivationFunctionType
ALU = mybir.AluOpType
AX = mybir.AxisListType

D = 1024
S = 8192
NB = 64
OWN = 2048
DFF = 2816
NFC = 22
ALPHA = 2.0 ** 0.25
MASKV = -240.0
NEG_BIG = -30000.0
ARENA_F32 = 51200


class Op:
    __slots__ = ("id", "eng", "emit", "deps", "is_dma", "sem", "val", "sig")

    def __init__(self, id, eng, emit, deps, is_dma):
        self.id = id
        self.eng = eng
        self.emit = emit
        self.deps = deps
        self.is_dma = is_dma
        self.sem = None
        self.val = 0
        self.sig = False


class Prog:
    ENGS = ("pe", "act", "dve", "pool", "sp")
    EPOCH = 6000
    NDMA = 20

    def __init__(self):
        self.ops = []
        self.last_w = {}
        self.readers = {}
        self.dma_readers = {}
        self.eng_last = {e: None for e in self.ENGS}
        self.pending = {e: set() for e in self.ENGS}
        self.dma_open = []
        self.dma_last_on_sem = [None] * self.NDMA
        self.dma_rr = 0
        self.dma_uses = [0] * self.NDMA

    def add(self, eng, emit, reads=(), writes=(), dma=False):
        deps = set()
        for k in reads:
            w = self.last_w.get(k)
            if w is not None:
                deps.add(w)
        for k in writes:
            w = self.last_w.get(k)
            if w is not None:
                deps.add(w)
            deps |= set(self.readers.get(k, {}).values())
            deps |= self.dma_readers.get(k, set())
        deps |= self.pending[eng]
        self.pending[eng] = set()
        op = Op(len(self.ops), eng, emit, deps, dma)
        if dma:
            s = self.dma_rr
            self.dma_rr = (self.dma_rr + 1) % self.NDMA
            prev = self.dma_last_on_sem[s]
            if prev is not None:
                op.deps.add(prev)
            self.dma_last_on_sem[s] = op.id
            self.dma_uses[s] += 1
            op.sem = ("dma", s)
            op.val = 16 * self.dma_uses[s]
            op.sig = True
            self.dma_open.append(op.id)
        op.deps.discard(op.id)
        self.ops.append(op)
        for k in reads:
            if dma:
                self.dma_readers.setdefault(k, set()).add(op.id)
            else:
                self.readers.setdefault(k, {})[eng] = op.id
        for k in writes:
            self.last_w[k] = op.id
            self.readers[k] = {}
            self.dma_readers[k] = set()
        if not dma:
            self.eng_last[eng] = op.id
        return op.id

    def barrier(self):
        deps = set(self.dma_open)
        self.dma_open = []
        for e in self.ENGS:
            if self.eng_last[e] is not None:
                deps.add(self.eng_last[e])
        for e in self.ENGS:
            self.pending[e] |= deps

    def finalize(self):
        ops = self.ops
        for op in ops:
            for d in op.deps:
                p = ops[d]
                if p.is_dma:
                    continue
                if p.eng == "pe" and op.eng == "pe" and not op.is_dma:
                    continue
                p.sig = True
        cnt = {e: 0 for e in self.ENGS}
        self.n_epochs = {e: 1 for e in self.ENGS}
        for op in ops:
            if op.is_dma or not op.sig:
                continue
            c = cnt[op.eng]
            ep = c // self.EPOCH
            op.sem = (op.eng, ep)
            op.val = c - ep * self.EPOCH + 1
            cnt[op.eng] = c + 1
            self.n_epochs[op.eng] = ep + 1

    def sem_names(self):
        names = [("dma", i) for i in range(self.NDMA)]
        for e in self.ENGS:
            for ep in range(self.n_epochs[e]):
                names.append((e, ep))
        return names

    def emit_engine(self, eng, engine, sems):
        waited = {}
        ops = self.ops
        for op in ops:
            if op.eng != eng:
                continue
            need = {}
            for d in op.deps:
                p = ops[d]
                if (not p.is_dma) and p.eng == "pe" and eng == "pe" and not op.is_dma:
                    continue
                if p.sem is None:
                    continue
                if need.get(p.sem, 0) < p.val:
                    need[p.sem] = p.val
            for sname, v in need.items():
                if waited.get(sname, 0) >= v:
                    continue
                engine.wait_ge(sems[sname], v)
                waited[sname] = v
            if op.emit is None:
                continue
            ins = op.emit(engine)
            if op.sig:
                ins.then_inc(sems[op.sem], 16 if op.is_dma else 1)


class Arena:
    def __init__(self, ap, nbytes):
        self.ap = ap
        self.top = 0
        self.cap = nbytes
        self.peak = 0

    def alloc(self, shape, dtype):
        n = 1
        for s in shape[1:]:
            n *= s
        esz = 2 if dtype == BF16 else 4
        nb = (n * esz + 31) // 32 * 32
        off = self.top
        self.top += nb
        self.peak = max(self.peak, self.top)
        assert self.top <= self.cap, ("arena overflow", self.top, self.cap)
        v = self.ap[:, off // 4:(off + nb) // 4]
        if dtype == BF16:
            v = v.bitcast(BF16)
        v = v[:, 0:n]
        if len(shape) == 3:
            v = v.rearrange("p (a b) -> p a b", a=shape[1])
        elif len(shape) == 4:
            v = v.rearrange("p (a b c) -> p a b c", a=shape[1], b=shape[2])
        return v

    def mark(self):
        return self.top

    def release(self, m):
        self.top = m


def build_program(stop=None):
    nc = bass.Bass("TRN2", target_bir_lowering=False)

    def din(name, shape, dt=F32):
        return nc.dram_tensor(name, list(shape), dt, kind="ExternalInput").ap()

    early = stop is not None and (stop in ("mod", "projA", "swa") or stop.startswith("swa"))
    xT = din("xT", [D, 128 if early else S])
    xq = din("xq", [D, OWN])
    xh = din("xh", [D, OWN])
    w_ada = din("w_ada", [D, 6 * D])
    w_in = din("w_in", [D, 2304])
    w_out = din("w_out", [D, 128 if early else D])
    w_gu = din("w_gu", [D, 128 if early else 2 * DFF])
    w_down = din("w_down", [DFF, 128 if early else D])
    cst_d = din("cst", [128, 5, 128], BF16)
    mk_d = din("mk", [128, 4, 128], BF16)
    vec_d = din("vec", [128, 128])
    bvb_d = din("bvb", [128, 768])
    swab_d = din("swab", [128, 2, 8, 256])
    outT = nc.dram_tensor("outT", [D, OWN], F32, kind="ExternalOutput").ap()

    arena_t = nc.alloc_sbuf_tensor("arena", [128, ARENA_F32], F32)
    A = Arena(arena_t[:, :], ARENA_F32 * 4)
    ps = nc.alloc_psum_tensor("ps", [128, 4096], F32)

    def bank(b, n=1):
        return ps[:, b * 512:(b + n) * 512]

    P = Prog()

    def finish():
        P.finalize()
        names = P.sem_names()
        from contextlib import ExitStack
        with ExitStack() as es:
            sems = {}
            for nm in names:
                sems[nm] = es.enter_context(nc.semaphore("s_%s_%d" % nm))
            block = es.enter_context(nc.Block())

            @block.tensor
            def _(e):
                P.emit_engine("pe", e, sems)

            @block.scalar
            def _(e):
                P.emit_engine("act", e, sems)

            @block.vector
            def _(e):
                P.emit_engine("dve", e, sems)

            @block.gpsimd
            def _(e):
                P.emit_engine("pool", e, sems)

            @block.sync
            def _(e):
                P.emit_engine("sp", e, sems)
        return nc, P, A


    outT_dbg = outT.rearrange("(kc p) t -> p kc t", p=128)

    def dbg_stop(dumps):
        P.barrier()
        col = 0
        keys = []
        for i, (v, n) in enumerate(dumps):
            P.add("sp", lambda e, v=v, n=n, col=col: e.dma_start(out=outT_dbg[:, 0, col:col + n], in_=v), [], [("dbg", i)], dma=True)
            keys.append(("dbg", i))
            col += n
        P.add("sp", None, reads=keys, writes=[])
        return finish()

    cst = A.alloc([128, 5, 128], BF16)
    ident = cst[:, 0, :]
    Umat = cst[:, 1, :]
    negones = cst[:, 2, :]
    onesM = cst[:, 3, :]
    ones512 = cst[:, 4, :]
    Mk = A.alloc([128, 4, 128], BF16)
    vec = A.alloc([128, 128], F32)
    b_in_c = vec[:, 0:18]
    gn_c = vec[:, 18:26]
    ln1g = vec[:, 26:34]
    ln1b = vec[:, 34:42]
    ln2g = vec[:, 42:50]
    ln2b = vec[:, 50:58]
    sinks_c = vec[:, 58:66]
    b_ada_c = vec[:, 66:114]
    c_c = vec[:, 114:122]
    modT = A.alloc([128, 48], F32)
    onep = A.alloc([128, 48], F32)
    silu_c = A.alloc([128, 8], F32)
    tmp8 = A.alloc([128, 8], F32)
    A_small_bf = A.alloc([128, 8], BF16)
    bvb = A.alloc([128, 768], F32)
    bq8 = A.alloc([128, 18], F32)
    wstage_flat = [A.alloc([128, 2048], F32) for _ in range(2)]
    wstage = [w.rearrange("p (a b) -> p a b", a=8) for w in wstage_flat]
    off_ys = A.mark()
    ysb = A.alloc([128, 4, OWN], BF16)
    ysw = A.alloc([128, 4, OWN], BF16)
    mQT = A.mark()
    QT = A.alloc([128, 4, OWN], BF16)

    sh_a = modT[:, 0:8]
    onep_a = onep[:, 8:16]
    onep_ga = onep[:, 16:24]
    sh_f = modT[:, 24:32]
    onep_f = onep[:, 32:40]
    onep_gf = onep[:, 40:48]

    def MM(out, lhsT, rhs, start, stop):
        return lambda e: e.matmul(out, lhsT=lhsT, rhs=rhs, start=start, stop=stop, skip_group_check=True)

    def TR(out, in_):
        return lambda e: e.transpose(out, in_, ident)

    def ACTF(out, in_, func, bias=None, scale=None):
        kw = {}
        if bias is not None:
            kw["bias"] = bias
        if scale is not None:
            kw["scale"] = scale
        return lambda e: e.activation(out=out, in_=in_, func=func, **kw)

    def TT(out, in0, in1, op):
        return lambda e: e.tensor_tensor(out=out, in0=in0, in1=in1, op=op)

    def TS(out, in0, s1, s2, op0, op1=None):
        if op1 is None:
            return lambda e: e.tensor_scalar(out=out, in0=in0, scalar1=s1, scalar2=None, op0=op0)
        return lambda e: e.tensor_scalar(out=out, in0=in0, scalar1=s1, scalar2=s2, op0=op0, op1=op1)

    def STT(out, in0, scalar, in1, op0, op1):
        return lambda e: e.scalar_tensor_tensor(out=out, in0=in0, scalar=scalar, in1=in1, op0=op0, op1=op1)

    def COPY(out, in_):
        return lambda e: e.tensor_copy(out=out, in_=in_)

    def RED(out, in_, op):
        return lambda e: e.tensor_reduce(out=out, in_=in_, axis=AX.X, op=op)

    def RECIP(out, in_):
        return lambda e: e.reciprocal(out=out, in_=in_)

    def MEMSET(out, v):
        return lambda e: e.memset(out, v)

    def DMA(out, in_):
        return lambda e: e.dma_start(out=out, in_=in_)

    def dma(out, in_, reads, writes):
        return P.add("sp", DMA(out, in_), reads=reads, writes=writes, dma=True)

    dma(cst, cst_d[:, :, :], [], ["cst"])
    dma(Mk, mk_d[:, :, :], [], ["mk"])
    dma(vec, vec_d[:, :], [], ["vec"])
    dma(bvb, bvb_d[:, :], [], ["bvb"])

    cv_rr = [0]

    def load_weight(dst, src_view, ncols, key, dup=False):
        for c0 in range(0, ncols, 256):
            w = min(256, ncols - c0)
            sl = cv_rr[0] % 2
            st = wstage[sl]
            dma(st[:, :, 0:w], src_view[:, :, c0:c0 + w], [], [("wst", sl)])
            eng = "dve" if (cv_rr[0] % 2 == 0) else "pool"
            cv_rr[0] += 1
            if not dup:
                P.add(eng, COPY(dst[:, :, c0:c0 + w], st[:, :, 0:w]), [("wst", sl)], [key])
            else:
                for g in range(w // 64):
                    for d2 in range(2):
                        P.add(eng, COPY(dst[:, :, g * 128 + d2 * 64: g * 128 + d2 * 64 + 64], st[:, :, g * 64:(g + 1) * 64]),
                              [("wst", sl)], [key])

    mod_rr = [0]

    def modulate(xs_t, xb_t, skey, bkey, onep_v, sh_v):
        for kc in range(8):
            eng = "dve" if (mod_rr[0] % 3 != 2) else "pool"
            mod_rr[0] += 1
            P.add(eng, TS(xb_t[:, kc, :], xs_t[:, kc, :], onep_v[:, kc:kc + 1], sh_v[:, kc:kc + 1], ALU.mult, ALU.add),
                  [skey, "modA"], [(bkey, kc)])

    bank_rr = [0]

    nbanks = [7]

    def next_bank():
        b = bank_rr[0] % nbanks[0]
        bank_rr[0] += 1
        return b

    w_in_v = w_in.rearrange("(kc p) n -> p kc n", p=128)
    xq_v = xq.rearrange("(kc p) t -> p kc t", p=128)
    xh_v = xh.rearrange("(kc p) t -> p kc t", p=128)
    xT_v = xT.rearrange("(kc p) t -> p kc t", p=128)
    chunk_id = [0]

    def stage_chunk(xs_l, xb_l, src):
        i = chunk_id[0]
        chunk_id[0] += 1
        sl = i % 2
        dma(xs_l[sl], src, [], [("xs", sl)])
        modulate(xs_l[sl], xb_l[sl], ("xs", sl), ("xb", sl), onep_a, sh_a)
        return xb_l[sl], [(("xb", sl), kc) for kc in range(8)]

    def proj_fm(xb_t, xkeys, W, wkey, c0):
        b = next_bank()
        for kc in range(8):
            P.add("pe", MM(bank(b), W[:, kc, c0:c0 + 128], xb_t[:, kc, :], kc == 0, kc == 7), [wkey, xkeys[kc]], [("bank", b)])
        return b

    def proj_tm(xb_t, xkeys, W, wkey, i, ncols):
        b = next_bank()
        for kc in range(8):
            P.add("pe", MM(bank(b)[:, 0:ncols], xb_t[:, kc, i * 128:(i + 1) * 128], W[:, kc, 0:ncols], kc == 0, kc == 7),
                  [wkey, xkeys[kc]], [("bank", b)])
        return b

    mA = A.mark()
    Wq = A.alloc([128, 8, 512], BF16)
    Wqs = A.alloc([128, 8, 512], BF16)
    Wks = A.alloc([128, 8, 256], BF16)
    Wvs = A.alloc([128, 8, 256], BF16)
    qsT = A.alloc([128, 4, OWN], BF16)
    kcat = A.alloc([128, 2, 16, 256], BF16)
    vsw = A.alloc([128, 16, 2, 256], BF16)
    m0 = A.mark()
    wa_stage = [A.alloc([128, 8, 512], F32) for _ in range(2)]
    wab0 = [A.alloc([128, 8, 512], BF16) for _ in range(2)]
    silu_bf = A_small_bf
    P.add("act", ACTF(tmp8, c_c, AF.Exp, scale=-1.0), ["vec"], ["tmp8"])
    P.add("dve", TS(tmp8, tmp8, 1.0, None, ALU.add), ["tmp8"], ["tmp8"])
    P.add("dve", RECIP(tmp8, tmp8), ["tmp8"], ["tmp8"])
    P.add("dve", TT(silu_c, c_c, tmp8, ALU.mult), ["tmp8", "vec"], ["silu"])
    P.add("dve", COPY(silu_bf, silu_c), ["silu"], ["silubf"])
    w_ada_v = w_ada.rearrange("(kc p) n -> p kc n", p=128)
    modp = bank(7)[:, 0:48]

    def mod_cols(stb, skey, col, c_in_piece):
        for kc in range(8):
            P.add("pe", MM(modp[:, col:col + 1], stb[:, kc, c_in_piece * 128:(c_in_piece + 1) * 128], silu_bf[:, kc:kc + 1], kc == 0, kc == 7),
                  [skey, "silubf"], [("modp", col)])

    for pc in range(4):
        st = wa_stage[pc % 2]
        dma(st, w_ada_v[:, :, pc * 512:(pc + 1) * 512], [], [("wa", pc % 2)])
        if pc == 1:
            load_weight(Wq, w_in_v[:, :, 0:512], 512, "Wq")
            load_weight(Wqs, w_in_v[:, :, 1536:2048], 512, "Wqs")
            load_weight(Wks, w_in_v[:, :, 2048:2176], 128, "Wks", dup=True)
            load_weight(Wvs, w_in_v[:, :, 2176:2304], 128, "Wvs", dup=True)
        P.add("dve", COPY(wab0[pc % 2], st), [("wa", pc % 2)], [("wab0", pc % 2)])
        for cc in range(4):
            mod_cols(wab0[pc % 2], ("wab0", pc % 2), pc * 4 + cc, cc)
    P.add("dve", TT(modT[:, 0:16], modp[:, 0:16], b_ada_c[:, 0:16], ALU.add), [("modp", c) for c in range(16)] + ["vec"], ["modA"])
    P.add("dve", TS(onep[:, 0:16], modT[:, 0:16], 1.0, None, ALU.add), ["modA"], ["modA"])
    P.add("dve", TS(bq8, b_in_c, 0.125, None, ALU.mult), ["vec"], ["bq8"])
    P.barrier()
    A.release(m0)
    if stop == "mod":
        return dbg_stop([(modT[:, 0:16], 16), (onep[:, 0:16], 16), (silu_c, 8)])

    mA2 = A.mark()
    xsA = [A.alloc([128, 8, 512], F32) for _ in range(2)]
    xbA = [A.alloc([128, 8, 512], BF16) for _ in range(2)]
    bk_sw = bvb[:, 512:768]

    for t in range(4):
        xb_t, xk = stage_chunk(xsA, xbA, xq_v[:, :, t * 512:(t + 1) * 512])
        for hp in range(4):
            b = proj_fm(xb_t, xk, Wq, "Wq", hp * 128)
            P.add("act", ACTF(QT[:, hp, t * 512:(t + 1) * 512], bank(b), AF.Identity, bias=bq8[:, hp:hp + 1], scale=0.125),
                  [("bank", b), "bq8"], [("QT", hp, t)])
        for hp in range(4):
            b = proj_fm(xb_t, xk, Wqs, "Wqs", hp * 128)
            P.add("act", ACTF(qsT[:, hp, t * 512:(t + 1) * 512], bank(b), AF.Identity, bias=b_in_c[:, 12 + hp:13 + hp], scale=1.0),
                  [("bank", b), "vec"], [("qsT", hp, t)])
        for g in range(2):
            b = proj_fm(xb_t, xk, Wks, "Wks", g * 128)
            P.add("dve", TS(kcat[:, g, 4 * t:4 * t + 4, 128:256], bank(b).rearrange("p (a b) -> p a b", a=4),
                            vec[:, 122 + g:123 + g], None, ALU.add), [("bank", b), "vec"], [("kcat", g, t, 1)])
        for i in range(4):
            b = proj_tm(xb_t, xk, Wvs, "Wvs", i, 256)
            P.add("dve", TT(vsw[:, 4 * t + i, 1, :], bank(b)[:, 0:256], bk_sw, ALU.add), [("bank", b), "bvb"], [("vsw", 4 * t + i, 1)])
    for t in range(4):
        xb_t, xk = stage_chunk(xsA, xbA, xh_v[:, :, t * 512:(t + 1) * 512])
        for g in range(2):
            b = proj_fm(xb_t, xk, Wks, "Wks", g * 128)
            P.add("dve", TS(kcat[:, g, 4 * t:4 * t + 4, 0:128], bank(b).rearrange("p (a b) -> p a b", a=4),
                            vec[:, 122 + g:123 + g], None, ALU.add), [("bank", b), "vec"], [("kcat", g, t, 0)])
        for i in range(4):
            b = proj_tm(xb_t, xk, Wvs, "Wvs", i, 256)
            P.add("dve", TT(vsw[:, 4 * t + i, 0, :], bank(b)[:, 0:256], bk_sw, ALU.add), [("bank", b), "bvb"], [("vsw", 4 * t + i, 0)])
    P.barrier()
    A.release(mA2)
    if stop == "projA":
        return dbg_stop([(modT, 48)])

    swab = A.alloc([128, 2, 8, 256], F32)
    dma(swab, swab_d[:, :, :, :], [], ["swab"])
    if stop == "swa0a":
        return dbg_stop([(swab[:, 0, 0, :], 256)])
    Ssb = [A.alloc([128, 4, 256], F32) for _ in range(2)]
    pn = [A.alloc([128, 4, 256], BF16) for _ in range(2)]
    pT = [A.alloc([128, 8, 128], BF16) for _ in range(2)]
    st4 = [A.alloc([128, 16], F32) for _ in range(2)]
    wab = [A.alloc([128, 8, 256], BF16) for _ in range(2)]
    def ACTF_ACC(out, in_, func, bias, acc):
        return lambda e: e.activation(out=out, in_=in_, func=func, bias=bias, scale=1.0, accum_out=acc)

    def swa_ctx(it):
        m, g = it // 2, it % 2
        sl = it % 2
        bS = (0, 1) if sl == 0 else (4, 5)
        bT = 2
        bO = 3 if sl == 0 else 6
        Sps = ps[:, bS[0] * 512:bS[0] * 512 + 1024].rearrange("p (a b) -> p a b", a=4)
        return m, g, sl, bS, bT, bO, Sps

    def swa_S(it):
        m, g, sl, bS, bT, bO, Sps = swa_ctx(it)
        for hh in range(4):
            h = 4 * g + hh
            pair, half = h // 2, h % 2
            a = half * 2 + hh // 2
            P.add("pe", MM(Sps[:, a, :], qsT[half * 64:(half + 1) * 64, pair, m * 128:(m + 1) * 128],
                           kcat[half * 64:(half + 1) * 64, g, m, :], True, True),
                  [("qsT", pair, m // 4), ("kcat", g, m // 4, 0), ("kcat", g, m // 4, 1)], [("bank", bS[half])])

    def swa_front(it):
        m, g, sl, bS, bT, bO, Sps = swa_ctx(it)
        Sv, sv = Ssb[sl], st4[sl]
        mx = sv[:, 0:4]
        rs = sv[:, 4:8]
        es = sv[:, 8:12]
        dd = sv[:, 12:16]
        skey = ("Ssb", sl)
        stk = ("st", sl)
        bsel = 0 if m == 0 else 1
        P.add("dve", STT(Sv, Sps, 0.125, swab[:, bsel, 4 * g:4 * g + 4, :], ALU.mult, ALU.add),
              [("bank", bS[0]), ("bank", bS[1]), "swab"], [skey])
        P.add("dve", RED(mx, Sv, ALU.max), [skey], [stk])
        P.add("dve", TT(mx, mx, sinks_c[:, 4 * g:4 * g + 4], ALU.max), [stk, "vec"], [stk])
        P.add("dve", TS(mx, mx, -1.0, None, ALU.mult), [stk], [stk])
        P.add("dve", TT(dd, sinks_c[:, 4 * g:4 * g + 4], mx, ALU.add), [stk, "vec"], [stk])
        for a in range(4):
            P.add("act", ACTF_ACC(Sv[:, a, :], Sv[:, a, :], AF.Exp, mx[:, a:a + 1], rs[:, a:a + 1]), [skey, stk], [skey, ("rs", sl)])
        P.add("act", ACTF(es, dd, AF.Exp), [stk, ("rs", sl)], [("es", sl)])

    def swa_back(it):
        m, g, sl, bS, bT, bO, Sps = swa_ctx(it)
        Sv, pnv, sv = Ssb[sl], pn[sl], st4[sl]
        rs = sv[:, 4:8]
        es = sv[:, 8:12]
        skey = ("Ssb", sl)
        P.add("dve", TT(rs, rs, es, ALU.add), [("es", sl), ("rs", sl)], [("rs", sl)])
        P.add("dve", RECIP(rs, rs), [("rs", sl)], [("rs", sl)])
        P.add("dve", TT(pnv, Sv, rs.unsqueeze(2).to_broadcast([128, 4, 256]), ALU.mult), [skey, ("rs", sl)], [("pn", sl)])

    def swa_tr(it):
        m, g, sl, bS, bT, bO, Sps = swa_ctx(it)
        pnv, pTv = pn[sl], pT[sl]
        Tps = bank(bT).bitcast(BF16)
        for a in range(4):
            for blk in range(2):
                idx = a * 2 + blk
                P.add("pe", TR(Tps[:, idx * 128:(idx + 1) * 128], pnv[:, a, blk * 128:(blk + 1) * 128]),
                      [("pn", sl), "cst"], [("bank", bT)])
        P.add("act", ACTF(pTv, Tps.rearrange("p (a b) -> p a b", a=8), AF.Copy), [("bank", bT)], [("pT", sl)])

    def swa_pv(it):
        m, g, sl, bS, bT, bO, Sps = swa_ctx(it)
        pTv = pT[sl]
        Ops = bank(bO)
        for a in range(4):
            for blk in range(2):
                P.add("pe", MM(Ops[:, a * 128:(a + 1) * 128], vsw[:, m, blk, g * 128:(g + 1) * 128], pTv[:, a * 2 + blk, :],
                               blk == 0, blk == 1), [("pT", sl), ("vsw", m, blk)], [("bank", bO)])
        Ov = Ops.rearrange("p (h t) -> p h t", h=4)
        P.add("act", ACTF(ysw[0:64, 2 * g:2 * g + 2, m * 128:(m + 1) * 128], Ov[0:64, 0:2, :], AF.Copy),
              [("bank", bO)], [("ysw", m, g, 0)])
        P.add("act", ACTF(ysw[64:128, 2 * g:2 * g + 2, m * 128:(m + 1) * 128], Ov[64:128, 2:4, :], AF.Copy),
              [("bank", bO)], [("ysw", m, g, 1)])

    swa_S(0)
    swa_front(0)
    swa_S(1)
    for it in range(32):
        if it + 1 < 32:
            swa_front(it + 1)
        swa_back(it)
        if it + 2 < 32:
            swa_S(it + 2)
        col = 16 + it
        wsl = (it // 2) % 2
        if it % 2 == 0:
            dma(wstage[wsl], w_ada_v[:, :, col * 128:col * 128 + 256], [], [("wst", wsl)])
            P.add("dve", COPY(wab[wsl], wstage[wsl]), [("wst", wsl)], [("wab", wsl)])
        mod_cols(wab[wsl], ("wab", wsl), col, it % 2)
        swa_tr(it)
        swa_pv(it)
    P.add("dve", TT(modT[:, 16:48], modp[:, 16:48], b_ada_c[:, 16:48], ALU.add), [("modp", c) for c in range(16, 48)] + ["vec"], ["modB"])
    P.add("dve", TS(onep[:, 16:48], modT[:, 16:48], 1.0, None, ALU.add), ["modB"], ["modB"])
    P.barrier()
    A.release(mA)
    nbanks[0] = 8
    if stop == "swa":
        return dbg_stop([(modT, 48)])

    for hf in range(2):
        mB = A.mark()
        KT = A.alloc([128, 2, S], BF16)
        Vt = A.alloc([128, NB, 256], BF16)
        mB2 = A.mark()
        Wk = A.alloc([128, 8, 256], BF16)
        Wv = A.alloc([128, 8, 256], BF16)
        xsB = [A.alloc([128, 8, 512], F32) for _ in range(2)]
        xbB = [A.alloc([128, 8, 512], BF16) for _ in range(2)]
        load_weight(Wk, w_in_v[:, :, 512 + hf * 256:512 + (hf + 1) * 256], 256, "Wk")
        load_weight(Wv, w_in_v[:, :, 1024 + hf * 256:1024 + (hf + 1) * 256], 256, "Wv")
        bv_sb = bvb[:, hf * 256:(hf + 1) * 256]
        for tc in range(16):
            xb_t, xk = stage_chunk(xsB, xbB, xT_v[:, :, tc * 512:(tc + 1) * 512])
            for lp in range(2):
                b = proj_fm(xb_t, xk, Wk, "Wk", lp * 128)
                P.add("act", ACTF(KT[:, lp, tc * 512:(tc + 1) * 512], bank(b), AF.Identity,
                                  bias=b_in_c[:, 4 + 2 * hf + lp:5 + 2 * hf + lp], scale=1.0), [("bank", b), "vec"], [("KT", lp, tc)])
            for i in range(4):
                b = proj_tm(xb_t, xk, Wv, "Wv", i, 256)
                P.add("dve", TT(Vt[:, 4 * tc + i, :], bank(b)[:, 0:256], bv_sb, ALU.add), [("bank", b), "bvb"], [("Vt", tc)])
        P.barrier()
        A.release(mB2)
        if stop == "projB":
            return dbg_stop([(modT, 48)])

        Esb = [A.alloc([128, 2, 512], F32) for _ in range(2)]
        Lsb = [A.alloc([128, 2, 512], BF16) for _ in range(2)]
        Tsb = [A.alloc([128, 2, 512], F32) for _ in range(3)]
        Psb = [A.alloc([128, 2, 512], BF16) for _ in range(2)]
        Rneg = [A.alloc([128, 2, 512], F32) for _ in range(2)]
        Ybk = [ps[:, 0:1024].rearrange("p (a b) -> p a b", a=2), ps[:, 1024:2048].rearrange("p (a b) -> p a b", a=2)]
        RB = ps[:, 2048:3072].rearrange("p (a b) -> p a b", a=2)
        Oab = [bank(6), bank(7)]

        def sb_stream(lp, hp, M, KT=KT, Vt=Vt, Esb=Esb, Lsb=Lsb, Tsb=Tsb, Psb=Psb, Rneg=Rneg):
            n = 16 * M + 16
            J = 16 * M + 15

            def info(k):
                j = J - k
                p = j - 16 * M
                q0 = (p // 4) if p >= 0 else 0
                return j, p, q0 * 128

            def emit_Z(k):
                j, p, c0 = info(k)
                s = k % 2
                for hd in range(2):
                    P.add("pe", MM(Ybk[s][:, hd, c0:512], KT[hd * 64:(hd + 1) * 64, lp, j * 128:(j + 1) * 128],
                                   QT[hd * 64:(hd + 1) * 64, hp, M * 512 + c0:M * 512 + 512], True, False),
                          [("KT", lp, j // 4), ("QT", hp, M)], [("Y", s)])
                if p >= 0:
                    q, ee = p // 4, p % 4
                    for hd in range(2):
                        P.add("pe", MM(Ybk[s][:, hd, q * 128:(q + 1) * 128], ident, Mk[:, ee, :], False, False),
                              ["cst", "mk"], [("Y", s)])

            def emit_E(k):
                j, p, c0 = info(k)
                s = k % 2
                P.add("act", ACTF(Esb[s][:, :, c0:512], Ybk[s][:, :, c0:512], AF.Exp), [("Y", s)], [("E", s)])

            def emit_L(k):
                j, p, c0 = info(k)
                s = k % 2
                P.add("act", ACTF(Lsb[s][:, :, c0:512], Esb[s][:, :, c0:512], AF.Ln, bias=1.0, scale=1.0), [("E", s)], [("L", s)])

            def emit_UO(k):
                j, p, c0 = info(k)
                s = k % 2
                for hd in range(2):
                    P.add("pe", MM(Ybk[s][:, hd, c0:512], Umat, Lsb[s][:, hd, c0:512], False, True), ["cst", ("L", s)], [("Y", s)])
                for hd in range(2):
                    P.add("pe", MM(RB[:, hd, c0:512], negones, Lsb[s][:, hd, c0:512], True, True), ["cst", ("L", s)], ["RB"])

            def emit_T(k):
                j, p, c0 = info(k)
                s = k % 2
                P.add("dve", TT(Tsb[k % 3][:, :, c0:512], Ybk[s][:, :, c0:512], Rneg[k % 2][:, :, c0:512], ALU.add),
                      [("Y", s), ("R", k % 2)], [("T", k % 3)])
                P.add("dve", TT(Rneg[(k + 1) % 2][:, :, c0:512], RB[:, :, c0:512], Rneg[k % 2][:, :, c0:512], ALU.add),
                      ["RB", ("R", k % 2)], [("R", (k + 1) % 2)])

            def emit_P(k):
                j, p, c0 = info(k)
                s = k % 2
                P.add("act", ACTF(Psb[s][:, :, c0:512], Tsb[k % 3][:, :, c0:512], AF.Exp), [("T", k % 3)], [("P", s)])

            def emit_PV(k):
                j, p, c0 = info(k)
                s = k % 2
                for hd in range(2):
                    P.add("pe", MM(Oab[hd][:, c0:512], Vt[:, j, lp * 128:(lp + 1) * 128], Psb[s][:, hd, c0:512], k == 0, k == n - 1),
                          [("Vt", j // 4), ("P", s)], [("O", hd)])

            P.add("pool", MEMSET(Rneg[0], 0.0), [], [("R", 0)])
            P.add("pool", MEMSET(Rneg[1], 0.0), [], [("R", 1)])
            emit_Z(0)
            for k in range(n + 2):
                if k < n:
                    emit_E(k)
                if k + 1 < n:
                    emit_Z(k + 1)
                zone_k = k < n and info(k)[1] >= 0
                if k - 2 >= 0 and not zone_k:
                    emit_P(k - 2)
                if k < n:
                    emit_L(k)
                    emit_UO(k)
                if k - 2 >= 0 and zone_k:
                    emit_P(k - 2)
                if k - 2 >= 0:
                    emit_PV(k - 2)
                if k < n:
                    emit_T(k)
            P.add("dve", COPY(ysb[0:64, hp, M * 512:(M + 1) * 512], Oab[0][0:64, :]), [("O", 0)], [("ysb", hp, M, 0)])
            P.add("dve", COPY(ysb[64:128, hp, M * 512:(M + 1) * 512], Oab[1][64:128, :]), [("O", 1)], [("ysb", hp, M, 1)])

        for lp in range(2):
            for M in range(4):
                sb_stream(lp, 2 * hf + lp, M)
        P.barrier()
        A.release(mB)
        if stop == "sb0":
            return dbg_stop([(modT, 48)])

    A.release(mQT)
    x1T = A.alloc([128, 8, OWN], F32)
    mX = A.mark()
    Wo = A.alloc([128, 8, 1024], BF16)
    load_weight(Wo, w_out.rearrange("(kc p) n -> p kc n", p=128), 1024, "Wo")
    xres = A.alloc([128, 8, 512], F32)
    mixed = A.alloc([128, 8, 512], BF16)
    sqbC = A.alloc([128, 8, 512], BF16)
    ubfC = A.alloc([128, 8, 512], BF16)
    sttC = [A.alloc([128, 512], F32) for _ in range(4)]

    def layer_norm_tile(tok0, gcols, bcols, ubf, sqb, stt):
        uv = x1T[:, :, tok0:tok0 + 512]
        ukeys = [("x1", tok0 // 512, c) for c in range(8)]
        P.add("dve", COPY(ubf, uv), ukeys, ["ubf"])
        P.add("act", ACTF(sqb, uv, AF.Square), ukeys, ["sqb"])
        bm, bq = next_bank(), next_bank()
        for c in range(8):
            P.add("pe", MM(bank(bm), onesM, ubf[:, c, :], c == 0, c == 7), ["cst", "ubf"], [("bank", bm)])
        for c in range(8):
            P.add("pe", MM(bank(bq), onesM, sqb[:, c, :], c == 0, c == 7), ["cst", "sqb"], [("bank", bq)])
        m2, var, rstd, nmr = stt
        P.add("act", ACTF(m2, bank(bm), AF.Square), [("bank", bm)], ["m2"])
        P.add("dve", TT(var, bank(bq), m2, ALU.subtract), [("bank", bq), "m2"], ["var"])
        P.add("act", ACTF(var, var, AF.Ln, bias=1e-5, scale=1.0), ["var"], ["var"])
        P.add("act", ACTF(rstd, var, AF.Exp, scale=-0.5), ["var"], ["rstd"])
        P.add("dve", STT(nmr, bank(bm), -1.0, rstd, ALU.mult, ALU.mult), [("bank", bm), "rstd"], ["nmr"])
        for c in range(8):
            eng = "dve"
            ukey = ukeys[c]
            P.add(eng, TT(uv[:, c, :], uv[:, c, :], rstd, ALU.mult), [ukey, "rstd"], [ukey])
            P.add(eng, TT(uv[:, c, :], uv[:, c, :], nmr, ALU.add), [ukey, "nmr"], [ukey])
            P.add("act", ACTF(uv[:, c, :], uv[:, c, :], AF.Identity, bias=bcols[:, c:c + 1], scale=gcols[:, c:c + 1]), [ukey, "vec"], [ukey])

    for tt in range(4):
        tok0 = tt * 512
        dma(xres, xq_v[:, :, tok0:tok0 + 512], [], ["xres"])
        for G, yv in ((0, ysb), (1, ysw)):
            yt = yv[:, :, tok0:tok0 + 512]
            P.add("dve", TT(sqbC[:, 4 * G:4 * G + 4, :], yt, yt, ALU.mult), [], [("sq", G)])
            bms = next_bank()
            for c in range(4):
                P.add("pe", MM(bank(bms), ones512, sqbC[:, 4 * G + c, :], c == 0, c == 3), ["cst", ("sq", G)], [("bank", bms)])
            rt = sttC[G]
            P.add("act", ACTF(rt, bank(bms), AF.Ln, bias=1e-6, scale=1.0), [("bank", bms)], [("rt", G)])
            P.add("act", ACTF(rt, rt, AF.Exp, scale=-0.5), [("rt", G)], [("rt", G)])
            for c in range(4):
                P.add("dve", STT(mixed[:, 4 * G + c, :], yt[:, c, :], gn_c[:, 4 * G + c:4 * G + c + 1], rt, ALU.mult, ALU.mult),
                      [("rt", G), "vec"], [("mixed", 4 * G + c)])
        P.add("act", ACTF(xres, xres, AF.Identity, scale=ALPHA), ["xres"], ["xres"])
        for co in range(8):
            b = next_bank()
            for c in range(8):
                P.add("pe", MM(bank(b), Wo[:, c, co * 128:(co + 1) * 128], mixed[:, c, :], c == 0, c == 7), ["Wo", ("mixed", c)], [("bank", b)])
            P.add("dve", STT(x1T[:, co, tok0:tok0 + 512], bank(b), onep_ga[:, co:co + 1], xres[:, co, :], ALU.mult, ALU.add),
                  [("bank", b), "xres", "modB"], [("x1", tt, co)])
        layer_norm_tile(tok0, ln1g, ln1b, ubfC, sqbC, sttC)
    P.barrier()
    A.release(mX)
    if stop == "phC":
        return dbg_stop([(x1T[:, 0, :], 2048)])

    sqbD = A.alloc([128, 8, 512], BF16)
    ubfD = A.alloc([128, 8, 512], BF16)
    sttD = [A.alloc([128, 512], F32) for _ in range(4)]
    actT = A.alloc([128, NFC, 1024], BF16)
    Wd = [A.alloc([128, NFC, 128], BF16) for _ in range(2)]
    A2 = Arena(A.ap[:, off_ys // 4:(off_ys + 32768) // 4], 32768)
    h2s = A2.alloc([128, 2, 8, 512], BF16)
    Wgu = [A2.alloc([128, 8, 256], BF16) for _ in range(2)]
    sg = [A2.alloc([128, 512], F32) for _ in range(2)]
    w_gu_v = w_gu.rearrange("(kc p) n -> p kc n", p=128)
    w_down_v = w_down.rearrange("(fc p) n -> p fc n", p=128)
    outT_v = outT.rearrange("(kc p) t -> p kc t", p=128)
    wg_i = 0
    wd_i = 0
    sg_i = 0
    for TT_ in range(2):
        for sub in range(2):
            tok0 = TT_ * 1024 + sub * 512
            for c in range(8):
                eng = "dve" if c % 2 == 0 else "pool"
                P.add(eng, TS(h2s[:, sub, c, :], x1T[:, c, tok0:tok0 + 512], onep_f[:, c:c + 1], sh_f[:, c:c + 1], ALU.mult, ALU.add),
                      [("x1", tok0 // 512, c), "modB"], [("h2", sub, c)])
        for fc in range(NFC):
            sl = wg_i % 2
            wg_i += 1
            wsl = cv_rr[0] % 2
            cv_rr[0] += 1
            st = wstage[wsl]
            skey = ("wst", wsl)
            dma(st[:, :, 0:128], w_gu_v[:, :, fc * 128:(fc + 1) * 128], [], [skey])
            dma(st[:, :, 128:256], w_gu_v[:, :, DFF + fc * 128:DFF + (fc + 1) * 128], [], [skey])
            P.add("pool", COPY(Wgu[sl], st), [skey], [("Wgu", sl)])
            for sub in range(2):
                bg, bu = next_bank(), next_bank()
                for kc in range(8):
                    P.add("pe", MM(bank(bg), Wgu[sl][:, kc, 0:128], h2s[:, sub, kc, :], kc == 0, kc == 7),
                          [("Wgu", sl), ("h2", sub, kc)], [("bank", bg)])
                for kc in range(8):
                    P.add("pe", MM(bank(bu), Wgu[sl][:, kc, 128:256], h2s[:, sub, kc, :], kc == 0, kc == 7),
                          [("Wgu", sl), ("h2", sub, kc)], [("bank", bu)])
                ss = sg_i % 2
                sg_i += 1
                P.add("act", ACTF(sg[ss], bank(bg), AF.Silu), [("bank", bg)], [("sg", ss)])
                P.add("dve", TT(actT[:, fc, sub * 512:(sub + 1) * 512], sg[ss], bank(bu), ALU.mult), [("sg", ss), ("bank", bu)], [("act", fc, sub)])
        for sub in range(2):
            tok0 = TT_ * 1024 + sub * 512
            P.add("act", ACTF(x1T[:, :, tok0:tok0 + 512], x1T[:, :, tok0:tok0 + 512], AF.Identity, scale=ALPHA),
                  [("x1", tok0 // 512, c) for c in range(8)] + [("h2", sub, c) for c in range(8)],
                  [("x1", tok0 // 512, c) for c in range(8)])
        for co in range(8):
            sl = wd_i % 2
            wd_i += 1
            for h2_ in range(2):
                wsl = cv_rr[0] % 2
                cv_rr[0] += 1
                stv = wstage_flat[wsl][:, 0:11 * 128].rearrange("p (a b) -> p a b", a=11)
                dma(stv, w_down_v[:, h2_ * 11:(h2_ + 1) * 11, co * 128:(co + 1) * 128], [], [("wst", wsl)])
                P.add("pool" if h2_ == 0 else "dve", COPY(Wd[sl][:, h2_ * 11:(h2_ + 1) * 11, :], stv), [("wst", wsl)], [("Wd", sl, h2_)])
            for sub in range(2):
                tok0 = TT_ * 1024 + sub * 512
                b = next_bank()
                for fc in range(NFC):
                    P.add("pe", MM(bank(b), Wd[sl][:, fc, :], actT[:, fc, sub * 512:(sub + 1) * 512], fc == 0, fc == NFC - 1),
                          [("Wd", sl, fc // 11), ("act", fc, sub)], [("bank", b)])
                P.add("dve", STT(x1T[:, co, tok0:tok0 + 512], bank(b), onep_gf[:, co:co + 1], x1T[:, co, tok0:tok0 + 512], ALU.mult, ALU.add),
                      [("bank", b), ("x1", tok0 // 512, co), "modB"], [("x1", tok0 // 512, co)])
        for sub in range(2):
            tok0 = TT_ * 1024 + sub * 512
            layer_norm_tile(tok0, ln2g, ln2b, ubfD, sqbD, sttD)
            P.add("sp", DMA(outT_v[:, :, tok0:tok0 + 512], x1T[:, :, tok0:tok0 + 512]),
                  reads=[("x1", tok0 // 512, c) for c in range(8)], writes=[("out", tok0)], dma=True)
    P.add("sp", None, reads=[("out", t * 512) for t in range(4)], writes=[])
    print("arena peak bytes", A.peak, "n_ops", len(P.ops))
    return finish()


_CACHE = {}


def _consts():
    bf = ml_dtypes.bfloat16
    cst = np.zeros((128, 5, 128), np.float32)
    cst[:, 0, :] = np.eye(128)
    sp, s_ = np.meshgrid(np.arange(128), np.arange(128), indexing="ij")
    cst[:, 1, :] = -(sp >= s_).astype(np.float32)
    cst[:, 2, :] = -1.0
    cst[:, 3, :] = 1.0 / 1024.0
    cst[:, 4, :] = 1.0 / 512.0
    return cst.astype(bf)


def kernel(x, c, w_ada, b_ada, w_in, b_in, sinks, gn_sb, gn_swa, w_out,
           ln1_g, ln1_b, w_gu, w_down, ln2_g, ln2_b):
    bf = ml_dtypes.bfloat16
    x = np.asarray(x, np.float32)
    f = lambda a: np.ascontiguousarray(np.asarray(a, np.float32))
    if "nc" not in _CACHE:
        _CACHE["nc"] = build_program()
    nc, P, A = _CACHE["nc"]

    cst = _consts()
    col = lambda v: np.asarray(v, np.float32).reshape(-1, 128).T
    b_in0 = np.asarray(b_in[0], np.float32)
    shared = {
        "w_ada": f(w_ada[0]), "w_in": f(w_in[0]), "w_out": f(w_out[0]), "w_gu": f(w_gu[0]), "w_down": f(w_down[0]),
        "cst": cst,
    }
    bvb = np.zeros((128, 768), np.float32)
    bvb[:, 0:512] = b_in0[1024:1536][None, :]
    bvsw = b_in0[2176:2304].reshape(2, 64)
    bvb[:, 512:768] = np.concatenate([bvsw[0], bvsw[0], bvsw[1], bvsw[1]])[None, :]
    shared["bvb"] = bvb
    t_i = np.arange(128)[:, None]
    s_i = np.arange(256)[None, :]
    dist = (t_i + 128 - s_i).astype(np.float32)
    inband = (dist >= 0) & (dist < 128)
    slopes = np.exp2(-8.0 * np.arange(1, 9, dtype=np.float32) / 8.0)
    base = np.where(inband[None], -slopes[:, None, None] * dist[None], NEG_BIG).astype(np.float32)
    base = np.transpose(base, (1, 0, 2))
    HPERM = np.array([0, 2, 1, 3, 4, 6, 5, 7])
    base = np.ascontiguousarray(base[:, HPERM, :])
    nohalo = base.copy()
    nohalo[:, :, 0:128] = NEG_BIG

    in_maps = []
    for core in range(8):
        b, r = core // 4, core % 4
        xb_ = x[b]
        blocks = xb_.reshape(NB, 128, D)
        own = blocks[r::4]
        halo_idx = np.arange(16) * 4 + r - 1
        halo = np.zeros_like(own)
        valid = halo_idx >= 0
        halo[valid] = blocks[halo_idx[valid]]
        vecs = np.zeros((128, 128), np.float32)
        vecs[:, 0:18] = col(b_in0)
        vecs[:, 18:22] = col(gn_sb[0])
        vecs[:, 22:26] = col(gn_swa[0])
        vecs[:, 26:34] = col(ln1_g[0])
        vecs[:, 34:42] = col(ln1_b[0])
        vecs[:, 42:50] = col(ln2_g[0])
        vecs[:, 50:58] = col(ln2_b[0])
        vecs[:, 58:66] = np.asarray(sinks[0], np.float32)[HPERM][None, :]
        vecs[:, 66:114] = col(b_ada[0])
        vecs[:, 114:122] = col(c[b])
        bks = b_in0[2048:2176].reshape(2, 64)
        vecs[:, 122] = np.concatenate([bks[0], bks[0]])
        vecs[:, 123] = np.concatenate([bks[1], bks[1]])
        mk = np.zeros((128, 4, 128), np.float32)
        ss, tt = np.meshgrid(np.arange(128), np.arange(128), indexing="ij")
        for e in range(4):
            if e == r:
                mk[:, e, :] = np.where(ss >= tt, MASKV, 0.0)
            elif e > r:
                mk[:, e, :] = MASKV
        swab = np.stack([nohalo if r == 0 else base, base], axis=1)
        m = dict(shared)
        m.update({
            "xT": np.ascontiguousarray(xb_.T),
            "xq": np.ascontiguousarray(own.reshape(OWN, D).T),
            "xh": np.ascontiguousarray(halo.reshape(OWN, D).T),
            "mk": mk.astype(bf),
            "vec": vecs,
            "swab": np.ascontiguousarray(swab),
        })
        in_maps.append(m)

    if _CACHE.get("early"):
        for m_ in in_maps:
            for k_ in ("xT", "w_out", "w_gu", "w_down"):
                m_[k_] = np.ascontiguousarray(m_[k_][:, 0:128])
    res = run_bass_kernel_spmd(nc, in_maps, core_ids=list(range(8)))
    out = np.zeros((2, S, D), np.float32)
    for core in range(8):
        b, r = core // 4, core % 4
        o = np.asarray(res.results[core]["outT"], np.float32)
        out[b].reshape(NB, 128, D)[r::4] = o.T.reshape(16, 128, D)
    return out
```
